# Optimizing a Trainium2 kernel written in Bass

```python
import math
import jax, jax.numpy as jnp
from jax import lax
import numpy as np

D_MODEL = 1024
BATCH = 2
SEQ = 16384
DEPTH = 2

N_Q = 8
N_KV = 2
GROUP = N_Q // N_KV
HEAD_DIM = 64
ATTN_W = N_Q * HEAD_DIM
KV_W = N_KV * HEAD_DIM
WINDOW = 128
BLOCK = 128
CONV_C = D_MODEL // 2
CONV_K = 31
D_FF = 4 * D_MODEL
IN_W = ATTN_W + 2 * KV_W + 2 * CONV_C + 2 * D_MODEL
EPS = 1e-6
NEG = -1e30

kernel_name = "hybrid_swa_sink_alibi_conformer_conv_gated_block"


def rms_norm(x, g):
    xf = x.astype(jnp.float32)
    y = xf * lax.rsqrt(jnp.mean(xf * xf, axis=-1, keepdims=True) + EPS)
    return (y * g.astype(jnp.float32)).astype(x.dtype)


def layer_norm(x, g, b):
    xf = x.astype(jnp.float32)
    mu = jnp.mean(xf, axis=-1, keepdims=True)
    xc = xf - mu
    var = jnp.mean(xc * xc, axis=-1, keepdims=True)
    y = xc * lax.rsqrt(var + EPS) * g.astype(jnp.float32) + b.astype(jnp.float32)
    return y.astype(x.dtype)


def alibi_slopes():
    return jnp.asarray(2.0 ** (-8.0 * np.arange(1, N_Q + 1) / N_Q), dtype=jnp.float32)


def sliding_window_attention(q, k, v, sinks):
    B, S = q.shape[0], q.shape[1]
    nb = S // BLOCK
    qb = q.reshape(B, nb, BLOCK, N_KV, GROUP, HEAD_DIM)

    def band(t):
        t = t.reshape(B, S, N_KV, HEAD_DIM)
        tp = jnp.pad(t, ((0, 0), (BLOCK, 0), (0, 0), (0, 0))).reshape(B, nb + 1, BLOCK, N_KV, HEAD_DIM)
        return jnp.concatenate([tp[:, :-1], tp[:, 1:]], axis=2)

    kb, vb = band(k), band(v)
    scale = 1.0 / math.sqrt(HEAD_DIM)
    s = jnp.einsum('bnqkgd,bnskd->bkgnqs', qb, kb).astype(jnp.float32) * scale

    qi = jnp.arange(BLOCK)[:, None] + BLOCK
    kj = jnp.arange(2 * BLOCK)[None, :]
    dist = qi - kj
    key_pos = jnp.arange(nb)[:, None, None] * BLOCK + kj[None] - BLOCK
    valid = (dist >= 0)[None] & (dist < WINDOW)[None] & (key_pos >= 0)

    slopes = alibi_slopes().reshape(N_KV, GROUP)
    s = s - slopes[None, :, :, None, None, None] * dist.astype(jnp.float32)
    s = jnp.where(valid, s, NEG)

    sink = sinks.astype(jnp.float32).reshape(N_KV, GROUP)[None, :, :, None, None]
    m = jnp.maximum(jnp.max(s, axis=-1), sink)
    p = jnp.exp(s - m[..., None])
    denom = jnp.sum(p, axis=-1) + jnp.exp(sink - m)
    p = p / denom[..., None]
    o = jnp.einsum('bkgnqs,bnskd->bnqkgd', p.astype(v.dtype), vb)
    return o.reshape(B, S, ATTN_W)


def causal_depthwise_conv(u, w, b):
    y = lax.conv_general_dilated(
        u, w[:, None, :].astype(u.dtype), window_strides=(1,), padding=[(CONV_K - 1, 0)],
        dimension_numbers=('NWC', 'WIO', 'NWC'), feature_group_count=CONV_C)
    return y + b


def setup_inputs(seed: int = 0) -> dict:
    key = jax.random.key(seed)
    ks = jax.random.split(key, 20)
    f = jnp.float32
    nrm = lambda k, shp, s: jax.random.normal(k, shp, f) * s
    return {
        "x": nrm(ks[0], (BATCH, SEQ, D_MODEL), 1.0),
        "mix_norm_g": 1.0 + nrm(ks[1], (DEPTH, D_MODEL), 0.01),
        "w_in": nrm(ks[2], (DEPTH, D_MODEL, IN_W), D_MODEL ** -0.5),
        "b_in": nrm(ks[3], (DEPTH, IN_W), 0.02),
        "sinks": nrm(ks[4], (DEPTH, N_Q), 0.5),
        "conv_w": nrm(ks[5], (DEPTH, CONV_K, CONV_C), CONV_K ** -0.5),
        "conv_b": nrm(ks[6], (DEPTH, CONV_C), 0.02),
        "conv_ln_g": 1.0 + nrm(ks[7], (DEPTH, CONV_C), 0.01),
        "conv_ln_b": nrm(ks[8], (DEPTH, CONV_C), 0.01),
        "w_attn_proj": nrm(ks[9], (DEPTH, ATTN_W, D_MODEL), ATTN_W ** -0.5),
        "w_conv_proj": nrm(ks[10], (DEPTH, CONV_C, D_MODEL), CONV_C ** -0.5),
        "b_conv_proj": nrm(ks[11], (DEPTH, D_MODEL), 0.02),
        "w_out": nrm(ks[12], (DEPTH, D_MODEL, D_MODEL), D_MODEL ** -0.5),
        "mlp_norm_g": 1.0 + nrm(ks[13], (DEPTH, D_MODEL), 0.01),
        "w_mlp1": nrm(ks[14], (DEPTH, D_MODEL, D_FF), D_MODEL ** -0.5),
        "w_mlp2": nrm(ks[15], (DEPTH, D_FF, D_MODEL), D_FF ** -0.5),
        "final_norm_g": 1.0 + nrm(ks[16], (D_MODEL,), 0.01),
    }


def reference(x, mix_norm_g, w_in, b_in, sinks, conv_w, conv_b, conv_ln_g, conv_ln_b,
              w_attn_proj, w_conv_proj, b_conv_proj, w_out, mlp_norm_g, w_mlp1, w_mlp2,
              final_norm_g):
    splits = np.cumsum([ATTN_W, KV_W, KV_W, CONV_C, CONV_C, D_MODEL]).tolist()
    for l in range(DEPTH):
        h = rms_norm(x, mix_norm_g[l])
        proj = jnp.einsum('bsd,de->bse', h, w_in[l]) + b_in[l]
        q, k, v, glu_a, glu_b, gate_a, gate_c = jnp.split(proj, splits, axis=-1)

        attn = sliding_window_attention(q, k, v, sinks[l])
        br_a = jnp.einsum('bse,ed->bsd', attn, w_attn_proj[l])

        u = glu_a * jax.nn.sigmoid(glu_b)
        u = causal_depthwise_conv(u, conv_w[l], conv_b[l])
        u = jax.nn.silu(layer_norm(u, conv_ln_g[l], conv_ln_b[l]))
        br_c = jnp.einsum('bsc,cd->bsd', u, w_conv_proj[l]) + b_conv_proj[l]

        merged = jax.nn.sigmoid(gate_a) * br_a + jax.nn.sigmoid(gate_c) * br_c
        x = x + jnp.einsum('bsd,de->bse', merged, w_out[l])

        h2 = rms_norm(x, mlp_norm_g[l])
        a = jnp.square(jax.nn.relu(jnp.einsum('bsd,df->bsf', h2, w_mlp1[l])))
        x = x + jnp.einsum('bsf,fd->bsd', a, w_mlp2[l])
    return rms_norm(x, final_norm_g)
```

```python
import math
from contextlib import ExitStack

import numpy as np
import concourse.bass as bass
import concourse.mybir as mybir
from concourse.bass_utils import run_bass_kernel_spmd

F32 = mybir.dt.float32
BF16 = mybir.dt.bfloat16
AF = mybir.ActivationFunctionType
ALU = mybir.AluOpType

D = 1024
SEQ = 16384
BATCH = 2
L = 2
NQ = 8
HD = 64
CONVK = 31
DFF = 4096
INW = 3840
EPS = 1e-6
NCORES = 8
TOK_CORE = 4096
HALO = 256
SLAB = TOK_CORE + HALO
NG = 28
GW = 4096
MASKV = -1.0e4


class Sched:
    ENGS = ("pe", "act", "dve", "pool", "sp")

    def __init__(self, self_sync=False):
        self.items = {e: [] for e in self.ENGS}
        self.semval = {}
        self.seen = {e: {} for e in self.ENGS}
        self.buf = {}
        self.regions = {}
        self.self_sync = self_sync
        self.needed = {}
        self.incs = {}

    def op(self, eng, fn, reads=(), writes=(), sem=None, inc=1, regions=()):
        need = {}

        def req(s, v):
            if v > need.get(s, 0):
                need[s] = v

        for k in reads:
            st = self.buf.get(k)
            if st and st["w"]:
                req(*st["w"])
        for k in writes:
            st = self.buf.get(k)
            if st:
                if st["w"]:
                    req(*st["w"])
                for s, v in st["r"].items():
                    req(s, v)
        for rname, mode in regions:
            rg = self.regions.setdefault(rname, {"mode": mode, "cur": {}, "prev": {}})
            if rg["mode"] != mode:
                rg["prev"] = rg["cur"]
                rg["cur"] = {}
                rg["mode"] = mode
            for s, v in rg["prev"].items():
                req(s, v)
        semname = sem or eng
        waits = []
        for s, v in need.items():
            if self.seen[eng].get(s, 0) >= v:
                continue
            if s == eng and (eng == "pe" or not self.self_sync):
                continue
            waits.append((s, v))
            self.seen[eng][s] = v
            self.needed.setdefault(s, set()).add(v)
        self.semval[semname] = self.semval.get(semname, 0) + 1
        self.incs[semname] = inc
        tok = (semname, self.semval[semname])
        self.items[eng].append((waits, fn, semname, tok[1]))
        for k in reads:
            st = self.buf.setdefault(k, {"w": None, "r": {}})
            if tok[1] > st["r"].get(semname, 0):
                st["r"][semname] = tok[1]
        for k in writes:
            self.buf[k] = {"w": tok, "r": {}}
        for rname, mode in regions:
            rg = self.regions[rname]
            if tok[1] > rg["cur"].get(semname, 0):
                rg["cur"][semname] = tok[1]
        return tok

    def force_token(self, key, tok):
        self.buf[key]["w"] = tok

    def final_waits(self, eng, toks):
        waits = []
        for s, v in toks:
            if self.seen[eng].get(s, 0) < v:
                waits.append((s, v))
                self.needed.setdefault(s, set()).add(v)
        self.items[eng].append((waits, None, None, 0))

    def emit(self, block, sems):
        rank = {}
        for s, n in self.semval.items():
            inc = self.incs[s]
            if inc != 1:
                rank[s] = {q: q * inc for q in range(1, n + 1)}
                self.needed[s] = set(range(1, n + 1))
            else:
                r, c = {}, 0
                for q in sorted(self.needed.get(s, ())):
                    c += 1
                    r[q] = c
                rank[s] = r
        engmap = {"pe": block.tensor, "act": block.scalar, "dve": block.vector,
                  "pool": block.gpsimd, "sp": block.sync}
        for ename, deco in engmap.items():
            items = self.items[ename]

            def body(e, items=items):
                for waits, fn, semname, seq in items:
                    for s, v in waits:
                        e.wait_ge(sems[s], rank[s][v])
                    if fn is not None:
                        ins = fn(e)
                        if seq in self.needed.get(semname, ()):
                            ins.then_inc(sems[semname], self.incs[semname])

            deco(body)


def _perm_q():
    idx = []
    for j in range(4):
        idx += list(range(j * 64, (j + 1) * 64))
        idx += list(range((4 + j) * 64, (5 + j) * 64))
    return np.array(idx)


def _perm_in_cols():
    pq = _perm_q()
    q0, k0, v0, ga0, gb0, gta0, gtc0 = 0, 512, 640, 768, 1280, 1792, 2816
    cols = []
    cols += list(range(gb0, gb0 + 512))
    cols += list(range(ga0, ga0 + 512))
    cols += list(range(k0, k0 + 128))
    cols += list(range(v0, v0 + 128))
    cols += list(q0 + pq)
    cols += list(range(gta0, gta0 + 1024))
    cols += list(range(gtc0, gtc0 + 1024))
    return np.array(cols)


def _group_layout(W, gsz):
    K, N = W.shape
    KC, M = K // 128, N // 128
    A = W.reshape(KC, 128, M, 128).transpose(1, 2, 0, 3)
    ng = (M + gsz - 1) // gsz
    out = np.zeros((ng, 128, GW), np.float32)
    for g in range(ng):
        m0, m1 = g * gsz, min(M, (g + 1) * gsz)
        blk = A[:, m0:m1].reshape(128, -1)
        out[g, :, :blk.shape[1]] = blk
    return out


def _alibi_table():
    slopes = 2.0 ** (-8.0 * np.arange(1, NQ + 1) / NQ)
    kk = np.arange(128)[:, None]
    qi = np.arange(128)[None, :]
    tab = np.zeros((128, 2, 2, 4, 128), np.float32)
    for g in range(2):
        for c in range(2):
            dist = qi + 128 - c * 128 - kk
            valid = (dist >= 0) & (dist < 128)
            for j in range(4):
                h = 4 * g + j
                tab[:, g, c, j, :] = np.where(valid, -slopes[h] * dist, MASKV)
    return tab.reshape(128, 2048)


def _par_offsets():
    off, o = {}, 0
    for name, n in [("ident", 128), ("g1", L * 8), ("g2", L * 8), ("gf", 8), ("bin", L * 30),
                    ("cw", L * 4 * CONVK), ("cb", L * 4), ("lng", L * 4), ("lnb", L * 4),
                    ("bcp", L * 8), ("flag", 1), ("eps", 1)]:
        off[name] = o
        o += n
    off["_n"] = o
    return off


POFF = _par_offsets()


def _colmajor(v):
    n = v.shape[-1] // 128
    r = v.reshape(v.shape[:-1] + (n, 128))
    return np.moveaxis(r, -1, 0)


def build_program(cfg=None):
    cfg = cfg or {}
    NTILES = cfg.get("ntiles", 9)
    NS = cfg.get("ns", 4)
    dumps = cfg.get("dumps", {})
    nlayers = cfg.get("nlayers", L)
    CONV_OFF = cfg.get("conv_offload", False)
    SPLIT_FIRST = cfg.get("split_first", False)
    PE_CONV = cfg.get("pe_conv", [1, 2, 3])

    nc = bass.Bass("TRN2", target_bir_lowering=False)
    xs = nc.dram_tensor("xs", [SLAB, D], F32, kind="ExternalInput").ap()
    wcat = nc.dram_tensor("wcat", [L * NG * 128, GW], F32, kind="ExternalInput").ap()
    par_d = nc.dram_tensor("par", [128, POFF["_n"]], F32, kind="ExternalInput").ap()
    sk_d = nc.dram_tensor("sk", [128, L * 512], F32, kind="ExternalInput").ap()
    alb_d = nc.dram_tensor("alb", [128, 2048], F32, kind="ExternalInput").ap()
    out_d = nc.dram_tensor("out", [TOK_CORE, D], F32, kind="ExternalOutput").ap()
    wbf = nc.dram_tensor("wbf", [L * NG * 128, GW], BF16, kind="Internal").ap()
    dump_d = {}
    for name, shape in dumps.items():
        dump_d[name] = nc.dram_tensor("dbg_" + name, list(shape), F32, kind="ExternalOutput").ap()

    S = Sched(self_sync=cfg.get("self_sync", False))

    with ExitStack() as es:
        def T(name, shape, dt):
            return es.enter_context(nc.sbuf_tensor("sb_" + name, shape, dt))

        x = T("x", [128, 8, 512], F32)
        xin = T("xin", [128, 2, 1024], F32)
        yout = T("yout", [128, 2, 1024], F32)
        sq = T("sq", [128, 8, 512], BF16)
        h = T("h", [128, 8, 512], BF16)
        rs = T("rs", [128, 512], F32)
        rstd = T("rstd", [128, 512], F32)
        vTf = rs
        qT = T("qT", [128, 4, 512], BF16)
        kT = [T(f"kT{l}", [128, 640], BF16) for l in range(L)]
        vtm = [T(f"vtm{l}", [128, 5, 128], BF16) for l in range(L)]
        sig = T("sig", [128, 4, 512], BF16)
        u = [T(f"u{l}", [128, 4, 544], BF16) for l in range(L)]
        R = T("R", [128, 16384], BF16)
        a_ = R[:, :].rearrange("p (m t) -> p m t", m=32)
        sga = R[:, 0:4096].rearrange("p (m t) -> p m t", m=8)
        sgc = R[:, 4096:8192].rearrange("p (m t) -> p m t", m=8)
        acc = R[:, 8192:12288].bitcast(F32).rearrange("p (m t) -> p m t", m=4)
        ybf = R[:, 12288:14336].rearrange("p (m t) -> p m t", m=4)
        ysq = R[:, 14336:16384].rearrange("p (m t) -> p m t", m=4)
        pe_ = T("pe", [128, 2, 1024], BF16)
        p_ = T("p", [128, 3, 1024], BF16)
        dsum = T("dsum", [128, 1, 512], F32)
        rden = dsum
        attnT = T("attnT", [128, 4, 512], BF16)
        lt = T("lt", [128, 2, 512], F32)
        osb = T("osb", [128, 512], F32)
        lvar = T("lvar", [128, 512], F32)
        musq = lvar
        rl = lvar
        zs = T("zs", [128, 4, 512], BF16)
        t1 = T("t1", [128, 2, 512], F32)
        t2 = T("t2", [128, 2, 512], F32)
        merged = T("merged", [128, 8, 512], BF16)
        dgC = merged[:, :, :].rearrange("p m t -> p (m t)")[:, 0:CONVK * 128].rearrange("p (k c) -> p k c", k=CONVK)
        r_ = T("r", [128, 2, 512], BF16)
        dgA = T("dgA", [128, CONVK, 128], BF16)
        dgB = yout[:, :, :].rearrange("p a b -> p (a b)").bitcast(BF16)[:, 0:CONVK * 128].rearrange("p (k c) -> p k c", k=CONVK)
        dgs = {2: dgA, 3: dgB, 1: dgC}
        dgkeys = {2: ["dgA"], 3: [("yout", s_, h_) for s_ in range(2) for h_ in range(2)], 1: [("merged", m) for m in range(8)]}
        wring = T("wring", [128, NS, GW], BF16)
        par = T("par", [128, POFF["_n"]], F32)
        esf = T("esf", [128, L, 512], F32)
        Eall = T("Eall", [128, 2, 2048], BF16)
        identb = T("identb", [128, 128], BF16)
        lnscr = T("lnscr", [128, 2], F32)
        ones1 = T("ones1", [128, 128], BF16)
        onesD = T("onesD", [128, 128], BF16)
        onesC = T("onesC", [128, 128], BF16)
        ps = es.enter_context(nc.psum_tensor("psum_all", [128, 8, 512], F32))

        semnames = ["pe", "act", "dve", "pool", "sp", "par", "xi0", "xi1", "o0", "o1", "dbg"]
        semnames += [f"w{i}" for i in range(NS)] + [f"c{i}" for i in range(20)]
        sems = {n: es.enter_context(nc.semaphore("sem_" + n)) for n in semnames}

        def P(name, i=0, n=1):
            o = POFF[name] + i
            return par[:, o:o + n]

        ident = par[:, POFF["ident"]:POFF["ident"] + 128]

        cast_spans = [(g, g + 1) for g in range(8)] + [(g, g + 4) for g in range(8, L * NG, 4)]
        cast_of_group = {}
        for ci, (g0, g1) in enumerate(cast_spans):
            for g in range(g0, g1):
                cast_of_group[g] = ci

        def cast_dma(ci):
            g0, g1 = cast_spans[ci]
            S.op("pool", lambda e: e.dma_start(out=wbf[g0 * 128:g1 * 128, :], in_=wcat[g0 * 128:g1 * 128, :],
                                               max_dma_last_dim=8192),
                 writes=[("wbf", ci)], sem=f"c{ci}", inc=16)

        NCAST0 = cfg.get("ncast0", 14)
        for ci in range(NCAST0):
            cast_dma(ci)
        S.op("sp", lambda e: e.dma_start(out=par[:], in_=par_d), writes=["par"], sem="par", inc=16)
        S.op("sp", lambda e: e.dma_start(out=esf[:].rearrange("p a b -> p (a b)"), in_=sk_d), writes=["esf"], sem="par", inc=16)
        xflat = x[:, 0:4, :].rearrange("p a b -> p (a b)")
        S.op("sp", lambda e: e.dma_start(out=xflat, in_=alb_d), writes=[("x", c) for c in range(4)], sem="par", inc=16)
        S.force_token("par", ("par", 3))
        S.force_token("esf", ("par", 3))
        for c in range(4):
            S.force_token(("x", c), ("par", 3))
        S.op("act", lambda e: e.activation(out=esf[:], in_=esf[:], func=AF.Exp), reads=["esf"], writes=["esf"])
        S.op("act", lambda e: e.activation(out=Eall[:, 0, :], in_=xflat, func=AF.Exp),
             reads=[("x", c) for c in range(4)], writes=["E0"])
        E4 = Eall[:, :, :].rearrange("p v (g c n) -> p v g c n", g=2, c=2)
        S.op("dve", lambda e: e.tensor_scalar(out=E4[:, 1, :, 0, :], in0=E4[:, 0, :, 0, :], scalar1=P("flag"), scalar2=None,
                                              op0=ALU.mult), reads=["E0", "par"], writes=["E1a"])
        S.op("dve", lambda e: e.tensor_copy(out=E4[:, 1, :, 1, :], in_=E4[:, 0, :, 1, :]), reads=["E0"], writes=["E1b"])
        S.op("dve", lambda e: e.tensor_copy(out=identb[:], in_=ident), reads=["par"], writes=["identb"])
        S.op("pool", lambda e: e.memset(ones1[:], 1.0), writes=["ones1"])
        S.op("pool", lambda e: e.memset(onesD[:], 1.0 / D), writes=["onesD"])
        S.op("pool", lambda e: e.memset(onesC[:], 1.0 / 512), writes=["onesC"])
        for l in range(L):
            S.op("pool", lambda e, l=l: e.memset(kT[l][:, 0:128], 0.0), writes=[("kT", l)])
            S.op("pool", lambda e, l=l: e.memset(vtm[l][:, 0, :], 0.0), writes=[("vtm", l)])
            S.op("pool", lambda e, l=l: e.memset(u[l][:, :, 0:30], 0.0), writes=[("u", l, j) for j in range(4)])
        EKEYS = ["E0", "E1a", "E1b"]

        def tile_T(ti):
            return 256 if ti == 0 else 512

        def layer_mode(ti, l):
            if ti == 0 and l == nlayers - 1 and nlayers > 1:
                return "kvu"
            return "full"

        wseq = []
        for ti in range(NTILES):
            for l in range(nlayers):
                if layer_mode(ti, l) == "kvu":
                    wseq += [(l, 0, GW), (l, 1, GW), (l, 2, 2048)]
                else:
                    for gi in range(NG):
                        wseq.append((l, gi, 2048 if gi == 7 else GW))

        wst = {"issued": 0, "cons": 0}

        def w_issue():
            n = wst["issued"]
            l, gi, width = wseq[n]
            slot = n % NS
            row = (l * NG + gi) * 128
            S.op("sp", lambda e: e.dma_start(out=wring[:, slot, 0:width], in_=wbf[row:row + 128, 0:width]),
                 reads=[("wbf", cast_of_group[l * NG + gi])], writes=[("w", slot)], sem=f"w{slot}", inc=16)
            wst["issued"] += 1

        def w_slot(l, gi, ofs=0):
            n = wst["cons"] + ofs
            assert wseq[n][:2] == (l, gi), (wseq[n], l, gi)
            assert n < wst["issued"]
            return n % NS

        def w_release():
            wst["cons"] += 1
            if wst["issued"] < len(wseq):
                w_issue()

        for _ in range(min(NS, len(wseq))):
            w_issue()

        xblocks = []
        for ti in range(NTILES):
            row0 = 0 if ti == 0 else HALO + (ti - 1) * 512
            for b in range(tile_T(ti) // 128):
                xblocks.append(row0 + b * 128)
        xst = {"issued": 0, "cons": 0}

        def x_issue():
            n = xst["issued"]
            slot = n % 2
            row = xblocks[n]
            S.op("sp", lambda e: e.dma_start(out=xin[:, slot, :], in_=xs[row:row + 128, :]),
                 writes=[("xin", slot)], sem=f"xi{slot}", inc=16)
            xst["issued"] += 1

        for _ in range(2):
            x_issue()

        mmctr = [0]

        def mmbank():
            b = mmctr[0] % 3
            mmctr[0] += 1
            return b

        def mm_group(out_ap, pairs):
            def fn(e):
                n = len(pairs)
                ins = None
                for i, (a, b) in enumerate(pairs):
                    ins = e.matmul(out_ap, lhsT=a, rhs=b, start=(i == 0), stop=(i == n - 1))
                return ins
            return fn

        def mm_chain(bank, Tt, wslot, wofs, act, akeys, wkey):
            n = len(akeys)
            pairs = [(wring[:, wslot, (wofs + kc) * 128:(wofs + kc + 1) * 128], act[:, kc, 0:Tt]) for kc in range(n)]
            if firstk[0] and SPLIT_FIRST:
                firstk[0] = False
                for kc in range(n):
                    a_, b_ = pairs[kc]
                    S.op("pe", lambda e, a_=a_, b_=b_, kc=kc: e.matmul(ps[:, bank, 0:Tt], lhsT=a_, rhs=b_,
                                                                      start=(kc == 0), stop=(kc == n - 1)),
                         reads=[akeys[kc], wkey], writes=[("ps", bank)] if kc == n - 1 else [("psx", bank)] if False else [("ps", bank)])
            else:
                S.op("pe", mm_group(ps[:, bank, 0:Tt], pairs), reads=list(akeys) + [wkey], writes=[("ps", bank)])

        def kmajor_first(Tt, wslot, mis, act, akeys, wkey):
            banks = {mi: mmbank() for mi in mis}
            n = len(akeys)
            for kc in range(n):
                def fn(e, kc=kc):
                    ins = None
                    for mi in mis:
                        ins = e.matmul(ps[:, banks[mi], 0:Tt], lhsT=wring[:, wslot, (mi * n + kc) * 128:(mi * n + kc + 1) * 128],
                                       rhs=act[:, kc, 0:Tt], start=(kc == 0), stop=(kc == n - 1))
                    return ins
                S.op("pe", fn, reads=[akeys[kc], wkey], writes=[("ps", banks[mi]) for mi in mis])
            return banks

        KMAJOR = cfg.get("kmajor", True)
        BLOCK_STATS = cfg.get("block_stats", True)
        evctr = [0]
        dbgctr = [0]

        def dump(name, ap, keys):
            if name not in dump_d:
                return
            dst = dump_d.pop(name)
            S.op("pool", lambda e: e.dma_start(out=dst, in_=ap), reads=keys, sem="dbg", inc=16)
            dbgctr[0] += 16

        pre_stats = [False]
        pend = []
        STAT_LAG = cfg.get("stat_lag", 3)

        def stat_emit(m, Tt):
            S.op("pe", lambda e: e.matmul(ps[:, 3, 0:Tt], lhsT=onesD[:], rhs=sq[:, m, 0:Tt], start=(m == 0), stop=(m == 7)),
                 reads=[("sq", m), "onesD"], writes=[("ps", 3)])

        def resid_add(m, bank, Tt):
            S.op("dve", lambda e: e.tensor_tensor(
                out=x[:, m, 0:Tt], in0=x[:, m, 0:Tt], in1=ps[:, bank, 0:Tt], op=ALU.add),
                reads=[("ps", bank), ("x", m)], writes=[("x", m)])
            S.op("act", lambda e: e.activation(out=sq[:, m, 0:Tt], in_=x[:, m, 0:Tt], func=AF.Square),
                 reads=[("x", m)], writes=[("sq", m)])
            pend.append((m, Tt))
            while len(pend) > STAT_LAG:
                stat_emit(*pend.pop(0))
            if m == 7:
                while pend:
                    stat_emit(*pend.pop(0))
                pre_stats[0] = True

        def norm(Tt, gname, l, tag, pre=False):
            xk = [("x", c) for c in range(8)]
            sqk = [("sq", c) for c in range(8)]
            if not pre:
                S.op("act", lambda e: e.activation(out=sq[:, :, 0:Tt], in_=x[:, :, 0:Tt], func=AF.Square), reads=xk, writes=sqk)
                S.op("pe", mm_group(ps[:, 3, 0:Tt], [(onesD[:], sq[:, c, 0:Tt]) for c in range(8)]),
                     reads=sqk + ["onesD"], writes=[("ps", 3)])
            S.op("act", lambda e: e.activation(out=rs[:, 0:Tt], in_=ps[:, 3, 0:Tt], func=AF.Ln, bias=P("eps"), scale=1.0),
                 reads=[("ps", 3), "par"], writes=["rs"])
            S.op("act", lambda e: e.activation(out=rstd[:, 0:Tt], in_=rs[:, 0:Tt], func=AF.Exp, scale=-0.5),
                 reads=["rs"], writes=["rstd"])

        def norm_h(Tt, gname, l, pre=False):
            norm(Tt, gname, l, "h", pre=pre)
            if gname == "g1":
                S.op("act", lambda e: e.activation(out=lnscr[:, 1:2], in_=P("eps"), func=AF.Sigmoid), reads=["par"], writes=["lnscr2"])
            for c in range(8):
                gi = (l * 8 + c) if gname != "gf" else c
                S.op("dve", lambda e, c=c, gi=gi: e.scalar_tensor_tensor(
                    out=h[:, c, 0:Tt], in0=x[:, c, 0:Tt], scalar=P(gname, gi), in1=rstd[:, 0:Tt],
                    op0=ALU.mult, op1=ALU.mult), reads=[("x", c), "rstd", "par"], writes=[("h", c)])
            firstk[0] = True

        firstk = [False]
        RA = [("R", "A")]
        RB = [("R", "B")]
        blkctr = [0]
        outctr = [0]
        bufctr = [0]

        def do_tile(ti):
            Tt = tile_T(ti)
            NBt = Tt // 128
            bst_pending = []
            for b in range(NBt):
                n = xst["cons"]
                slot = n % 2
                for half in range(2):
                    bank = mmbank()
                    def tr(e, slot=slot, half=half, bank=bank):
                        ins = None
                        for cc in range(4):
                            c = half * 4 + cc
                            ins = e.transpose(ps[:, bank, cc * 128:(cc + 1) * 128], xin[:, slot, c * 128:(c + 1) * 128], ident)
                        return ins
                    S.op("pe", tr, reads=[("xin", slot), "par"], writes=[("ps", bank)])
                    dst = x[:, half * 4:half * 4 + 4, b * 128:(b + 1) * 128]
                    src = ps[:, bank, :].rearrange("p (c t) -> p c t", c=4)
                    wk = [("x", half * 4 + cc) for cc in range(4)]
                    sqd = sq[:, half * 4:half * 4 + 4, b * 128:(b + 1) * 128]
                    sqk = [("sq", half * 4 + cc) for cc in range(4)]
                    if half == 0:
                        S.op("act", lambda e, dst=dst, src=src: e.activation(out=dst, in_=src, func=AF.Copy),
                             reads=[("ps", bank)], writes=wk)
                        if BLOCK_STATS:
                            S.op("act", lambda e, dst=dst, sqd=sqd: e.activation(out=sqd, in_=dst, func=AF.Square),
                                 reads=wk, writes=sqk)
                    else:
                        S.op("dve", lambda e, dst=dst, src=src: e.tensor_copy(out=dst, in_=src),
                             reads=[("ps", bank)], writes=wk)
                        if BLOCK_STATS:
                            S.op("dve", lambda e, dst=dst, sqd=sqd: e.tensor_tensor(out=sqd, in0=dst, in1=dst, op=ALU.mult),
                                 reads=wk, writes=sqk)
                xst["cons"] += 1
                if xst["issued"] < len(xblocks):
                    x_issue()
                if BLOCK_STATS:
                    bc = slice(b * 128, (b + 1) * 128)
                    xk8 = [("x", c) for c in range(8)]
                    sqk8 = [("sq", c) for c in range(8)]
                    def bst(e, bc=bc):
                        ins = None
                        for c in range(8):
                            ins = e.matmul(ps[:, 3, bc], lhsT=onesD[:], rhs=sq[:, c, bc], start=(c == 0), stop=(c == 7))
                        return ins
                    if len(bst_pending) >= 2:
                        S.op("pe", bst_pending.pop(0), reads=sqk8 + ["onesD"], writes=[("ps", 3)])
                    bst_pending.append(bst)
            if BLOCK_STATS:
                while bst_pending:
                    S.op("pe", bst_pending.pop(0), reads=[("sq", c) for c in range(8)] + ["onesD"], writes=[("ps", 3)])
                pre_stats[0] = True
            if ti == 0:
                dump("x0", x[:, :, 0:256], [("x", c) for c in range(8)])

            def do_layer(l):
                mode = layer_mode(ti, l)
                full = mode == "full"
                norm_h(Tt, "g1", l, pre=pre_stats[0])
                pre_stats[0] = False
                pending_builds = []
                if full:
                    for j in PE_CONV:
                        for k in range(CONVK):
                            pending_builds.append((j, k))

                def emit_builds(n):
                    for _ in range(min(n, len(pending_builds))):
                        j, k = pending_builds.pop(0)
                        S.op("dve", lambda e, j=j, k=k: e.tensor_scalar(
                            out=dgs[j][:, k, :], in0=identb[:], scalar1=P("cw", (l * 4 + j) * CONVK + k), scalar2=None,
                            op0=ALU.mult),
                            reads=["identb", "par"], writes=[("dg", j, k)] + (dgkeys[j] if k == 0 else []), regions=())

                emit_builds(16)
                if ti == 0 and l == 0:
                    dump("h0", h[:, :, 0:256], [("h", c) for c in range(8)])
                hk = [("h", c) for c in range(8)]
                mlist = list(range(30)) if full else list(range(10))
                ngroups = 8 if full else 3

                def gen_in():
                    for gi in range(ngroups):
                        slot = w_slot(l, gi)
                        pre = {}
                        if firstk[0] and KMAJOR:
                            firstk[0] = False
                            pre = kmajor_first(Tt, slot, [0, 1, 2], h, hk, ("w", slot))
                        for mi in range(4):
                            m = gi * 4 + mi
                            if m not in mlist:
                                continue
                            if mi in pre:
                                bank = pre[mi]
                            else:
                                bank = mmbank()
                                mm_chain(bank, Tt, slot, mi * 8, h, hk, ("w", slot))
                            src = ps[:, bank, 0:Tt]
                            bias = P("bin", l * 30 + m)
                            if m < 4:
                                S.op("act", lambda e, src=src, bias=bias, m=m: e.activation(
                                    out=sig[:, m, 0:Tt], in_=src, func=AF.Sigmoid, bias=bias, scale=1.0),
                                    reads=[("ps", bank), "par"], writes=[("sig", m)])
                            elif m < 8:
                                j = m - 4
                                S.op("dve", lambda e, src=src, bias=bias, j=j: e.scalar_tensor_tensor(
                                    out=u[l][:, j, 30:30 + Tt], in0=src, scalar=bias, in1=sig[:, j, 0:Tt],
                                    op0=ALU.add, op1=ALU.mult),
                                    reads=[("ps", bank), "par", ("sig", j)], writes=[("u", l, j)])
                            elif m == 8:
                                S.op("act", lambda e, src=src, bias=bias: e.activation(
                                    out=kT[l][:, 128:128 + Tt], in_=src, func=AF.Identity, bias=bias, scale=1.0),
                                    reads=[("ps", bank), "par"], writes=[("kT", l)])
                            elif m == 9:
                                S.op("act", lambda e, src=src, bias=bias: e.activation(
                                    out=vTf[:, 0:Tt], in_=src, func=AF.Identity, bias=bias, scale=1.0),
                                    reads=[("ps", bank), "par"], writes=["vTf"])
                                vb = mmbank()
                                def vtr(e, vb=vb):
                                    ins = None
                                    for b in range(NBt):
                                        ins = e.transpose(ps[:, vb, b * 128:(b + 1) * 128], vTf[:, b * 128:(b + 1) * 128], ident)
                                    return ins
                                S.op("pe", vtr, reads=["vTf", "par"], writes=[("ps", vb)])
                                S.op("dve", lambda e, vb=vb: e.tensor_copy(
                                    out=vtm[l][:, 1:1 + NBt, :], in_=ps[:, vb, 0:Tt].rearrange("p (b f) -> p b f", b=NBt)),
                                    reads=[("ps", vb)], writes=[("vtm", l)])
                            elif m < 14:
                                j = m - 10
                                S.op("act", lambda e, src=src, bias=bias, j=j: e.activation(
                                    out=qT[:, j, 0:Tt], in_=src, func=AF.Identity, bias=bias, scale=1.0),
                                    reads=[("ps", bank), "par"], writes=[("qT", j)])
                            else:
                                gt, i = (sga, m - 14) if m < 22 else (sgc, m - 22)
                                gk = ("sga", i) if m < 22 else ("sgc", i)
                                if m % 2 == 0:
                                    S.op("act", lambda e, src=src, bias=bias, gt=gt, i=i: e.activation(
                                        out=gt[:, i, 0:Tt], in_=src, func=AF.Identity, bias=bias, scale=1.0),
                                        reads=[("ps", bank), "par"], writes=[gk], regions=RA)
                                else:
                                    S.op("dve", lambda e, src=src, bias=bias, gt=gt, i=i: e.tensor_scalar(
                                        out=gt[:, i, 0:Tt], in0=src, scalar1=bias, scalar2=None, op0=ALU.add),
                                        reads=[("ps", bank), "par"], writes=[gk], regions=RA)
                            if mi == 3 or m == mlist[-1]:
                                w_release()
                            emit_builds(11)
                            yield

                def att_front(nb, g, bf, pb):
                    gs = slice(g * 64, (g + 1) * 64)
                    var = 1 if (ti == 1 and nb == 0) else 0
                    qk = [("qT", j) for j in range(4)]
                    def smm(e):
                        e.matmul(ps[:, 4, :].rearrange("p (j q) -> p j q", j=4), lhsT=kT[l][gs, nb * 128:(nb + 1) * 128],
                                 rhs=qT[gs, :, nb * 128:(nb + 1) * 128], start=True, stop=True)
                        return e.matmul(ps[:, 5, :].rearrange("p (j q) -> p j q", j=4),
                                        lhsT=kT[l][gs, (nb + 1) * 128:(nb + 2) * 128],
                                        rhs=qT[gs, :, nb * 128:(nb + 1) * 128], start=True, stop=True)
                    S.op("pe", smm, reads=qk + [("kT", l)], writes=[("ps", 4), ("ps", 5)])
                    S.op("act", lambda e: e.activation(
                        out=pe_[:, pb, :], in_=ps[:, 4:6, :].rearrange("p a b -> p (a b)"), func=AF.Exp, scale=0.125),
                        reads=[("ps", 4), ("ps", 5)], writes=[("pe", pb)])
                    S.op("dve", lambda e: e.tensor_tensor(
                        out=p_[:, bf, :], in0=pe_[:, pb, :], in1=Eall[:, var, g * 1024:(g + 1) * 1024], op=ALU.mult),
                        reads=[("pe", pb)] + EKEYS, writes=[("p", bf)])

                def att_pv(nb, g, bf):
                    gs = slice(g * 64, (g + 1) * 64)
                    def pv(e):
                        e.matmul(ps[gs, 6, :], lhsT=vtm[l][:, nb, gs], rhs=p_[:, bf, 0:512], start=True, stop=False)
                        e.matmul(ps[gs, 6, :], lhsT=vtm[l][:, nb + 1, gs], rhs=p_[:, bf, 512:1024], start=False, stop=True)
                        e.matmul(ps[gs, 7, :], lhsT=ones1[:, 0:64], rhs=p_[:, bf, 0:512], start=True, stop=False)
                        return e.matmul(ps[gs, 7, :], lhsT=ones1[:, 0:64], rhs=p_[:, bf, 512:1024], start=False, stop=True)
                    S.op("pe", pv, reads=[("p", bf), ("vtm", l), "ones1"], writes=[("ps", 6, g), ("ps", 7, g)])

                def att_norm(nb, db):
                    S.op("dve", lambda e: e.tensor_tensor(
                        out=dsum[:, db, :], in0=ps[:, 7, :], in1=esf[:, l, :], op=ALU.add),
                        reads=[("ps", 7, 0), ("ps", 7, 1), "esf"], writes=[("dsum", db)])
                    S.op("act", lambda e: e.activation(out=osb[:], in_=ps[:, 6, :], func=AF.Copy),
                         reads=[("ps", 6, 0), ("ps", 6, 1)], writes=["osb"])
                    S.op("act", lambda e: e.activation(out=dsum[:, db, :], in_=dsum[:, db, :], func=AF.Ln),
                         reads=[("dsum", db)], writes=[("dsum", db)])
                    S.op("act", lambda e: e.activation(out=rden[:, db, :], in_=dsum[:, db, :], func=AF.Exp, scale=-1.0),
                         reads=[("dsum", db)], writes=[("dsum", db)])
                    S.op("dve", lambda e: e.tensor_tensor(
                        out=attnT[:, :, nb * 128:(nb + 1) * 128],
                        in0=osb[:].rearrange("p (j q) -> p j q", j=4),
                        in1=rden[:, db, :].rearrange("p (j q) -> p j q", j=4), op=ALU.mult),
                        reads=["osb", ("dsum", db)], writes=[("attnT", 0), ("attnT", 1)])

                def gen_att():
                    steps = [(nb, g) for nb in range(NBt) for g in range(2)]
                    bfs = []
                    for _ in steps:
                        bfs.append(bufctr[0] % 3)
                        bufctr[0] += 1
                    att_front(steps[0][0], steps[0][1], bfs[0], 0)
                    yield
                    att_front(steps[1][0], steps[1][1], bfs[1], 1)
                    yield
                    for i, (nb, g) in enumerate(steps):
                        if i + 2 < len(steps):
                            att_front(steps[i + 2][0], steps[i + 2][1], bfs[i + 2], i % 2)
                        att_pv(nb, g, bfs[i])
                        if g == 1:
                            att_norm(nb, 0)
                        yield

                def gen_conv():
                    for k in range(CONVK):
                        for j in range(4):
                            if j in PE_CONV:
                                continue
                            wk_ = P("cw", (l * 4 + j) * CONVK + k)
                            cb_ = P("cb", l * 4 + j)
                            if j < 2 or not CONV_OFF:
                                if k == 0:
                                    S.op("dve", lambda e, j=j, wk_=wk_, cb_=cb_: e.tensor_scalar(
                                        out=acc[:, j, 0:Tt], in0=u[l][:, j, 0:Tt], scalar1=wk_, scalar2=cb_,
                                        op0=ALU.mult, op1=ALU.add),
                                        reads=[("u", l, j), "par"], writes=[("acc", j)], regions=RA)
                                else:
                                    S.op("dve", lambda e, j=j, k=k, wk_=wk_: e.scalar_tensor_tensor(
                                        out=acc[:, j, 0:Tt], in0=u[l][:, j, k:k + Tt], scalar=wk_, in1=acc[:, j, 0:Tt],
                                        op0=ALU.mult, op1=ALU.add),
                                        reads=[("u", l, j), ("acc", j), "par"], writes=[("acc", j)], regions=RA)
                            else:
                                if k == 0:
                                    S.op("act", lambda e, j=j, wk_=wk_, cb_=cb_: e.activation(
                                        out=acc[:, j, 0:Tt], in_=u[l][:, j, 0:Tt], func=AF.Identity, scale=wk_, bias=cb_),
                                        reads=[("u", l, j), "par"], writes=[("acc", j)], regions=RA)
                                else:
                                    sl = (j - 2) * 2 + (k % 2)
                                    S.op("act", lambda e, j=j, k=k, wk_=wk_, sl=sl: e.activation(
                                        out=tmpc[:, sl, 0:Tt], in_=u[l][:, j, k:k + Tt], func=AF.Identity, scale=wk_),
                                        reads=[("u", l, j), "par"], writes=[("tmpc", sl)])
                                    S.op("pool", lambda e, j=j, sl=sl: e.tensor_tensor(
                                        out=acc[:, j, 0:Tt], in0=acc[:, j, 0:Tt], in1=tmpc[:, sl, 0:Tt], op=ALU.add),
                                        reads=[("tmpc", sl), ("acc", j)], writes=[("acc", j)], regions=RA)
                        yield

                def step(g):
                    try:
                        next(g)
                        return g
                    except StopIteration:
                        return None

                def gen_ln():
                    acck = [("acc", j) for j in range(4)]
                    dvej = [j for j in range(4) if j not in PE_CONV]
                    pej = [j for j in range(4) if j in PE_CONV]
                    for j in dvej:
                        S.op("act", lambda e, j=j: e.activation(out=ybf[:, j, 0:Tt], in_=acc[:, j, 0:Tt], func=AF.Copy),
                             reads=[("acc", j)], writes=[("ybf", j)], regions=RA)
                        S.op("act", lambda e, j=j: e.activation(out=ysq[:, j, 0:Tt], in_=acc[:, j, 0:Tt], func=AF.Square),
                             reads=[("acc", j)], writes=[("ysq", j)], regions=RA)
                    yield
                    m2b = mmbank()
                    order = pej + dvej
                    def ln_mm(e, m2b=m2b, js=None, first=True, last=True):
                        ins = None
                        for i, j in enumerate(js):
                            ins = e.matmul(ps[:, 3, 0:Tt], lhsT=onesC[:], rhs=ybf[:, j, 0:Tt], start=(first and i == 0),
                                           stop=(last and i == len(js) - 1))
                        for i, j in enumerate(js):
                            ins = e.matmul(ps[:, m2b, 0:Tt], lhsT=onesC[:], rhs=ysq[:, j, 0:Tt], start=(first and i == 0),
                                           stop=(last and i == len(js) - 1))
                        return ins
                    if pej and dvej:
                        S.op("pe", lambda e: ln_mm(e, js=pej, first=True, last=False),
                             reads=[("ybf", j) for j in pej] + [("ysq", j) for j in pej] + ["onesC"],
                             writes=[("ps", 3), ("ps", m2b)], regions=RA)
                        S.op("pe", lambda e: ln_mm(e, js=dvej, first=False, last=True),
                             reads=[("ybf", j) for j in dvej] + [("ysq", j) for j in dvej] + ["onesC"],
                             writes=[("ps", 3), ("ps", m2b)], regions=RA)
                    else:
                        S.op("pe", lambda e: ln_mm(e, js=order, first=True, last=True),
                             reads=[("ybf", j) for j in order] + [("ysq", j) for j in order] + ["onesC"],
                             writes=[("ps", 3), ("ps", m2b)], regions=RA)
                    S.op("act", lambda e: e.activation(out=musq[:, 0:Tt], in_=ps[:, 3, 0:Tt], func=AF.Square),
                         reads=[("ps", 3)], writes=["lvar"])
                    S.op("dve", lambda e, m2b=m2b: e.scalar_tensor_tensor(out=lvar[:, 0:Tt], in0=ps[:, m2b, 0:Tt], scalar=EPS,
                                                                         in1=musq[:, 0:Tt], op0=ALU.add, op1=ALU.subtract),
                         reads=[("ps", m2b), "lvar"], writes=["lvar"])
                    yield
                    S.op("act", lambda e: e.activation(out=lvar[:, 0:Tt], in_=lvar[:, 0:Tt], func=AF.Ln),
                         reads=["lvar"], writes=["lvar"])
                    S.op("act", lambda e: e.activation(out=rl[:, 0:Tt], in_=lvar[:, 0:Tt], func=AF.Exp, scale=-0.5),
                         reads=["lvar"], writes=["lvar"])
                    yield
                    for j in range(4):
                        if j == 2:
                            yield
                        bf = j % 2
                        S.op("dve", lambda e, j=j, bf=bf: e.tensor_tensor(out=lt[:, bf, 0:Tt], in0=acc[:, j, 0:Tt], in1=ps[:, 3, 0:Tt],
                                                                          op=ALU.subtract),
                             reads=[("acc", j), ("ps", 3)], writes=[("lt", bf)], regions=RA)
                        S.op("dve", lambda e, bf=bf: e.tensor_tensor(out=lt[:, bf, 0:Tt], in0=lt[:, bf, 0:Tt], in1=rl[:, 0:Tt],
                                                                     op=ALU.mult),
                             reads=[("lt", bf), "lvar"], writes=[("lt", bf)])
                        S.op("act", lambda e, j=j, bf=bf: e.activation(
                            out=zs[:, j, 0:Tt], in_=lt[:, bf, 0:Tt], func=AF.Silu,
                            bias=P("lnb", l * 4 + j), scale=P("lng", l * 4 + j)),
                            reads=[("lt", bf), "par"], writes=[("zs", j)])

                gin = gen_in()
                n_in = 0
                for _ in range(8):
                    gin = step(gin)
                    n_in += 1
                gcv = gen_conv() if full else None
                gat = None
                att_started = False
                pe_conv_done = not (full and PE_CONV)
                ln_done = False
                gln = None

                def pe_conv():
                    for j in PE_CONV:
                        bank = mmbank()
                        S.op("pe", mm_group(ps[:, bank, 0:Tt], [(dgs[j][:, k, :], u[l][:, j, k:k + Tt]) for k in range(CONVK)]),
                             reads=[("u", l, j)] + [("dg", j, k) for k in range(CONVK)] + dgkeys[j], writes=[("ps", bank)],
                             regions=())
                        S.op("act", lambda e, j=j, bank=bank: e.activation(
                            out=acc[:, j, 0:Tt], in_=ps[:, bank, 0:Tt], func=AF.Identity, bias=P("cb", l * 4 + j), scale=1.0),
                            reads=[("ps", bank), "par"], writes=[("acc", j)], regions=RA)
                        S.op("act", lambda e, j=j, bank=bank: e.activation(
                            out=ybf[:, j, 0:Tt], in_=ps[:, bank, 0:Tt], func=AF.Identity, bias=P("cb", l * 4 + j), scale=1.0),
                            reads=[("ps", bank), "par"], writes=[("ybf", j)], regions=RA)
                        S.op("act", lambda e, j=j, bank=bank: e.activation(
                            out=ysq[:, j, 0:Tt], in_=ps[:, bank, 0:Tt], func=AF.Square, bias=P("cb", l * 4 + j), scale=1.0),
                            reads=[("ps", bank), "par"], writes=[("ysq", j)], regions=RA)

                while gin or gcv or gat or gln:
                    if gin:
                        gin = step(gin)
                        n_in += 1
                    if not pe_conv_done and n_in >= cfg.get("pe_conv_at", 12):
                        emit_builds(1000)
                        pe_conv()
                        pe_conv_done = True
                    if full and not att_started and n_in >= 14:
                        gat = gen_att()
                        att_started = True
                    if gat:
                        gat = step(gat)
                    if gcv:
                        gcv = step(gcv)
                    if gcv:
                        gcv = step(gcv)
                    if full and not ln_done and gcv is None and pe_conv_done:
                        gln = gen_ln()
                        ln_done = True
                    if gln:
                        gln = step(gln)
                if ti == 0 and l == 0:
                    dump("q0", qT[:, :, 0:256], [("qT", j) for j in range(4)])
                    dump("k0", kT[0][:, 0:384], [("kT", 0)])
                    dump("v0", vtm[0][:, 0:3, :], [("vtm", 0)])
                    dump("u0", u[0][:, :, 0:286], [("u", 0, j) for j in range(4)])
                    if full:
                        dump("attn0", attnT[:, :, 0:256], [("attnT", 0), ("attnT", 1)])
                S.op("pool", lambda e: e.tensor_copy(out=kT[l][:, 0:128], in_=kT[l][:, Tt:Tt + 128]),
                     reads=[("kT", l)], writes=[("kT", l)])
                S.op("pool", lambda e: e.tensor_copy(out=vtm[l][:, 0, :], in_=vtm[l][:, NBt, :]),
                     reads=[("vtm", l)], writes=[("vtm", l)])
                uk = [("u", l, j) for j in range(4)]
                if ti == 0:
                    S.op("pool", lambda e: e.tensor_scalar(
                        out=u[l][:, :, 0:30], in0=u[l][:, :, Tt:Tt + 30], scalar1=P("flag"), scalar2=None, op0=ALU.mult),
                        reads=uk + ["par"], writes=uk)
                else:
                    S.op("pool", lambda e: e.tensor_copy(out=u[l][:, :, 0:30], in_=u[l][:, :, Tt:Tt + 30]),
                         reads=uk, writes=uk)

                if not full:
                    return
                if not ln_done:
                    gln = gen_ln()
                while gln:
                    gln = step(gln)
                if ti == 0 and l == 0:
                    dump("acc0", acc[:, :, 0:256], [("acc", j) for j in range(4)])
                if ti == 0 and l == 0:
                    dump("zs0", zs[:, :, 0:256], [("zs", j) for j in range(4)])

                for m in range(2):
                    S.op("act", lambda e, m=m: e.activation(out=sga[:, m, 0:Tt], in_=sga[:, m, 0:Tt], func=AF.Sigmoid),
                         reads=[("sga", m)], writes=[("sga", m)], regions=RA)
                    S.op("act", lambda e, m=m: e.activation(out=sgc[:, m, 0:Tt], in_=sgc[:, m, 0:Tt], func=AF.Sigmoid),
                         reads=[("sgc", m)], writes=[("sgc", m)], regions=RA)
                sA = w_slot(l, 8, 0)
                sC = w_slot(l, 9, 1)
                for m in range(8):
                    bA = mmbank()
                    S.op("pe", mm_group(ps[:, bA, 0:Tt],
                                        [(wring[:, sA, (m * 4 + c) * 128:(m * 4 + c + 1) * 128], attnT[:, c, 0:Tt]) for c in range(4)]),
                         reads=[("attnT", 0), ("attnT", 1), ("w", sA)], writes=[("ps", bA)])
                    bC = mmbank()
                    S.op("pe", mm_group(ps[:, bC, 0:Tt],
                                        [(wring[:, sC, (m * 4 + c) * 128:(m * 4 + c + 1) * 128], zs[:, c, 0:Tt]) for c in range(4)]),
                         reads=[("zs", j) for j in range(4)] + [("w", sC)], writes=[("ps", bC)])
                    bf = m % 2
                    if m + 2 < 8:
                        S.op("act", lambda e, m=m: e.activation(out=sga[:, m + 2, 0:Tt], in_=sga[:, m + 2, 0:Tt], func=AF.Sigmoid),
                             reads=[("sga", m + 2)], writes=[("sga", m + 2)], regions=RA)
                        S.op("act", lambda e, m=m: e.activation(out=sgc[:, m + 2, 0:Tt], in_=sgc[:, m + 2, 0:Tt], func=AF.Sigmoid),
                             reads=[("sgc", m + 2)], writes=[("sgc", m + 2)], regions=RA)
                    else:
                        if m == 6:
                            S.op("act", lambda e: e.activation(out=lnscr[:, 0:1], in_=P("eps"), func=AF.Ln),
                                 reads=["par"], writes=["lnscr"])
                    S.op("dve", lambda e, bA=bA, bf=bf, m=m: e.tensor_tensor(
                        out=t1[:, bf, 0:Tt], in0=ps[:, bA, 0:Tt], in1=sga[:, m, 0:Tt], op=ALU.mult),
                        reads=[("ps", bA), ("sga", m)], writes=[("t1", bf)], regions=RA)
                    S.op("dve", lambda e, bC=bC, bf=bf, m=m: e.scalar_tensor_tensor(
                        out=t2[:, bf, 0:Tt], in0=ps[:, bC, 0:Tt], scalar=P("bcp", l * 8 + m), in1=sgc[:, m, 0:Tt],
                        op0=ALU.add, op1=ALU.mult),
                        reads=[("ps", bC), ("sgc", m), "par"], writes=[("t2", bf)], regions=RA)
                    S.op("pool", lambda e, bf=bf, m=m: e.tensor_tensor(
                        out=merged[:, m, 0:Tt], in0=t1[:, bf, 0:Tt], in1=t2[:, bf, 0:Tt], op=ALU.add),
                        reads=[("t1", bf), ("t2", bf)], writes=[("merged", m)])
                w_release()
                w_release()
                if ti == 0 and l == 0:
                    dump("sga0", sga[:, :, 0:256], [("sga", i) for i in range(8)])
                    dump("merged0", merged[:, :, 0:256], [("merged", m) for m in range(8)])

                mk = [("merged", m) for m in range(8)]
                for gi in (10, 11):
                    slot = w_slot(l, gi)
                    pre = {}
                    if gi == 10 and KMAJOR:
                        pre = kmajor_first(Tt, slot, [0, 1, 2], merged, mk, ("w", slot))
                    for mi in range(4):
                        m = (gi - 10) * 4 + mi
                        if mi in pre:
                            bank = pre[mi]
                        else:
                            bank = mmbank()
                            S.op("pe", mm_group(ps[:, bank, 0:Tt],
                                                [(wring[:, slot, (mi * 8 + c) * 128:(mi * 8 + c + 1) * 128], merged[:, c, 0:Tt])
                                                 for c in range(8)]),
                                 reads=mk + [("w", slot)], writes=[("ps", bank)])
                        resid_add(m, bank, Tt)
                    w_release()
                if ti == 0 and l == 0:
                    dump("x1", x[:, :, 0:256], [("x", c) for c in range(8)])

                norm_h(Tt, "g2", l, pre=pre_stats[0])
                pre_stats[0] = False
                for gi in range(12, 20):
                    slot = w_slot(l, gi)
                    pre = {}
                    if firstk[0] and KMAJOR:
                        firstk[0] = False
                        pre = kmajor_first(Tt, slot, [0, 1, 2], h, hk, ("w", slot))
                    for mi in range(4):
                        m = (gi - 12) * 4 + mi
                        if mi in pre:
                            bank = pre[mi]
                        else:
                            bank = mmbank()
                            mm_chain(bank, Tt, slot, mi * 8, h, hk, ("w", slot))
                        bf = m % 2
                        S.op("act", lambda e, bank=bank, bf=bf: e.activation(out=r_[:, bf, 0:Tt], in_=ps[:, bank, 0:Tt], func=AF.Relu),
                             reads=[("ps", bank)], writes=[("r", bf)])
                        S.op("pool", lambda e, bf=bf, m=m: e.tensor_tensor(
                            out=a_[:, m, 0:Tt], in0=r_[:, bf, 0:Tt], in1=r_[:, bf, 0:Tt], op=ALU.mult),
                            reads=[("r", bf)], writes=[("a", m)], regions=RB)
                    w_release()
                ak = [("a", m) for m in range(32)]
                for gi in range(20, 28):
                    slot = w_slot(l, gi)
                    m = gi - 20
                    bank = mmbank()
                    if gi == 20:
                        def part(e, c0, c1, bank=bank, slot=slot):
                            ins = None
                            for c in range(c0, c1):
                                ins = e.matmul(ps[:, bank, 0:Tt], lhsT=wring[:, slot, c * 128:(c + 1) * 128], rhs=a_[:, c, 0:Tt],
                                               start=(c == 0), stop=(c == 31))
                            return ins
                        S.op("pe", lambda e, part=part: part(e, 0, 26), reads=ak[0:26] + [("w", slot)], writes=[("ps", bank)],
                             regions=RB)
                        S.op("pe", lambda e, part=part: part(e, 26, 32), reads=ak[26:32] + [("w", slot)], writes=[("ps", bank)],
                             regions=RB)
                    else:
                        S.op("pe", mm_group(ps[:, bank, 0:Tt],
                                            [(wring[:, slot, c * 128:(c + 1) * 128], a_[:, c, 0:Tt]) for c in range(32)]),
                             reads=ak + [("w", slot)], writes=[("ps", bank)], regions=RB)
                    resid_add(m, bank, Tt)
                    w_release()
                if ti == 0 and l == 0:
                    dump("x2", x[:, :, 0:256], [("x", c) for c in range(8)])

            for l in range(nlayers):
                do_layer(l)
                if ti == 0 and l == 0:
                    for ci in range(NCAST0, len(cast_spans)):
                        cast_dma(ci)
            if ti == 0:
                return
            norm(Tt, "gf", 0, "f", pre=pre_stats[0])
            pre_stats[0] = False
            for c in range(8):
                S.op("dve", lambda e, c=c: e.scalar_tensor_tensor(
                    out=x[:, c, 0:Tt], in0=x[:, c, 0:Tt], scalar=P("gf", c), in1=rstd[:, 0:Tt],
                    op0=ALU.mult, op1=ALU.mult), reads=[("x", c), "rstd", "par"], writes=[("x", c)])
            for b in range(NBt):
                oslot = outctr[0] % 2
                outctr[0] += 1
                for half in range(2):
                    bank = mmbank()
                    def otr(e, half=half, bank=bank, b=b):
                        ins = None
                        for cc in range(4):
                            c = half * 4 + cc
                            ins = e.transpose(ps[:, bank, cc * 128:(cc + 1) * 128], x[:, c, b * 128:(b + 1) * 128], ident)
                        return ins
                    S.op("pe", otr, reads=[("x", half * 4 + cc) for cc in range(4)] + ["par"], writes=[("ps", bank)])
                    dst = yout[:, oslot, half * 512:(half + 1) * 512]
                    if half == 0:
                        S.op("act", lambda e, dst=dst, bank=bank: e.activation(out=dst, in_=ps[:, bank, :], func=AF.Copy),
                             reads=[("ps", bank)], writes=[("yout", oslot, half)])
                    else:
                        S.op("dve", lambda e, dst=dst, bank=bank: e.tensor_copy(out=dst, in_=ps[:, bank, :]),
                             reads=[("ps", bank)], writes=[("yout", oslot, half)])
                row = (ti - 1) * 512 + b * 128
                S.op("act", lambda e, oslot=oslot, row=row: e.dma_start(out=out_d[row:row + 128, :], in_=yout[:, oslot, :]),
                     reads=[("yout", oslot, 0), ("yout", oslot, 1)], sem=f"o{oslot}", inc=16)

        for ti in range(NTILES):
            do_tile(ti)

        fin = [(s, S.semval[s]) for s in ("o0", "o1", "dbg") if s in S.semval]
        S.final_waits("act", fin)
        with nc.Block() as block:
            S.emit(block, sems)
    return nc


def prepare_inputs(x, mix_norm_g, w_in, b_in, sinks, conv_w, conv_b, conv_ln_g, conv_ln_b,
                   w_attn_proj, w_conv_proj, b_conv_proj, w_out, mlp_norm_g, w_mlp1, w_mlp2, final_norm_g):
    f = lambda a: np.asarray(a, dtype=np.float32)
    x = f(x)
    pc = _perm_in_cols()
    pq = _perm_q()
    groups = []
    for l in range(L):
        groups.append(_group_layout(f(w_in[l])[:, pc], 4))
        groups.append(_group_layout(f(w_attn_proj[l])[pq, :], 8))
        groups.append(_group_layout(f(w_conv_proj[l]), 8))
        groups.append(_group_layout(f(w_out[l]), 4))
        groups.append(_group_layout(f(w_mlp1[l]), 4))
        groups.append(_group_layout(f(w_mlp2[l]), 1))
    wcat = np.ascontiguousarray(np.concatenate(groups, axis=0).reshape(L * NG * 128, GW))
    par = np.zeros((128, POFF["_n"]), np.float32)

    def put(name, arr):
        arr = np.asarray(arr, np.float32).reshape(128, -1)
        par[:, POFF[name]:POFF[name] + arr.shape[1]] = arr

    put("ident", np.eye(128, dtype=np.float32))
    put("g1", _colmajor(f(mix_norm_g)))
    put("g2", _colmajor(f(mlp_norm_g)))
    put("gf", _colmajor(f(final_norm_g)))
    put("bin", _colmajor(f(b_in)[:, pc]))
    cw = f(conv_w)
    cwp = _colmajor(cw)
    put("cw", np.transpose(cwp, (0, 1, 3, 2)))
    put("cb", _colmajor(f(conv_b)))
    put("lng", _colmajor(f(conv_ln_g)))
    put("lnb", _colmajor(f(conv_ln_b)))
    put("bcp", _colmajor(f(b_conv_proj)))
    par[:, POFF["eps"]] = EPS
    sk4 = f(sinks).reshape(L, 2, 4)
    sk = np.zeros((128, L, 4, 128), np.float32)
    for g in range(2):
        sk[g * 64:(g + 1) * 64] = sk4[None, :, g, :, None]
    sk = np.ascontiguousarray(sk.reshape(128, L * 512))
    alb = _alibi_table()
    in_maps = []
    for c in range(NCORES):
        b = c // 4
        t0 = (c % 4) * TOK_CORE
        slab = np.zeros((SLAB, D), np.float32)
        if t0 == 0:
            slab[HALO:] = x[b, 0:TOK_CORE]
        else:
            slab[:] = x[b, t0 - HALO:t0 + TOK_CORE]
        pcore = par.copy()
        pcore[:, POFF["flag"]] = 0.0 if t0 == 0 else 1.0
        in_maps.append({"xs": slab, "wcat": wcat, "par": pcore, "sk": sk, "alb": alb})
    return in_maps


def kernel(**inputs):
    in_maps = prepare_inputs(**inputs)
    nc = build_program()
    res = run_bass_kernel_spmd(nc, in_maps, core_ids=list(range(NCORES)))
    out = np.zeros((BATCH, SEQ, D), np.float32)
    for c in range(NCORES):
        b = c // 4
        t0 = (c % 4) * TOK_CORE
        out[b, t0:t0 + TOK_CORE] = res.results[c]["out"]
    return out
```

```python
import math
from contextlib import ExitStack

import numpy as np
import concourse.bass as bass
import concourse.mybir as mybir
from concourse.bass_utils import run_bass_kernel_spmd

F32 = mybir.dt.float32
BF16 = mybir.dt.bfloat16
AF = mybir.ActivationFunctionType
ALU = mybir.AluOpType

D = 1024
SEQ = 16384
BATCH = 2
L = 2
NQ = 8
HD = 64
CONVK = 31
DFF = 4096
INW = 3840
EPS = 1e-6
NCORES = 8
TOK_CORE = 4096
HALO = 256
SLAB = TOK_CORE + HALO
NG = 28
GW = 4096
MASKV = -1.0e4


class Sched:
    ENGS = ("pe", "act", "dve", "pool", "sp")

    def __init__(self, self_sync=False):
        self.items = {e: [] for e in self.ENGS}
        self.semval = {}
        self.seen = {e: {} for e in self.ENGS}
        self.buf = {}
        self.regions = {}
        self.self_sync = self_sync
        self.needed = {}
        self.incs = {}

    def op(self, eng, fn, reads=(), writes=(), sem=None, inc=1, regions=()):
        need = {}

        def req(s, v):
            if v > need.get(s, 0):
                need[s] = v

        for k in reads:
            st = self.buf.get(k)
            if st and st["w"]:
                req(*st["w"])
        for k in writes:
            st = self.buf.get(k)
            if st:
                if st["w"]:
                    req(*st["w"])
                for s, v in st["r"].items():
                    req(s, v)
        for rname, mode in regions:
            rg = self.regions.setdefault(rname, {"mode": mode, "cur": {}, "prev": {}})
            if rg["mode"] != mode:
                rg["prev"] = rg["cur"]
                rg["cur"] = {}
                rg["mode"] = mode
            for s, v in rg["prev"].items():
                req(s, v)
        semname = sem or eng
        waits = []
        for s, v in need.items():
            if self.seen[eng].get(s, 0) >= v:
                continue
            if s == eng and (eng == "pe" or not self.self_sync):
                continue
            waits.append((s, v))
            self.seen[eng][s] = v
            self.needed.setdefault(s, set()).add(v)
        self.semval[semname] = self.semval.get(semname, 0) + 1
        self.incs[semname] = inc
        tok = (semname, self.semval[semname])
        self.items[eng].append((waits, fn, semname, tok[1]))
        for k in reads:
            st = self.buf.setdefault(k, {"w": None, "r": {}})
            if tok[1] > st["r"].get(semname, 0):
                st["r"][semname] = tok[1]
        for k in writes:
            self.buf[k] = {"w": tok, "r": {}}
        for rname, mode in regions:
            rg = self.regions[rname]
            if tok[1] > rg["cur"].get(semname, 0):
                rg["cur"][semname] = tok[1]
        return tok

    def force_token(self, key, tok):
        self.buf[key]["w"] = tok

    def final_waits(self, eng, toks):
        waits = []
        for s, v in toks:
            if self.seen[eng].get(s, 0) < v:
                waits.append((s, v))
                self.needed.setdefault(s, set()).add(v)
        self.items[eng].append((waits, None, None, 0))

    def emit(self, block, sems):
        rank = {}
        for s, n in self.semval.items():
            inc = self.incs[s]
            if inc != 1:
                rank[s] = {q: q * inc for q in range(1, n + 1)}
                self.needed[s] = set(range(1, n + 1))
            else:
                r, c = {}, 0
                for q in sorted(self.needed.get(s, ())):
                    c += 1
                    r[q] = c
                rank[s] = r
        engmap = {"pe": block.tensor, "act": block.scalar, "dve": block.vector,
                  "pool": block.gpsimd, "sp": block.sync}
        for ename, deco in engmap.items():
            items = self.items[ename]

            def body(e, items=items):
                for waits, fn, semname, seq in items:
                    for s, v in waits:
                        e.wait_ge(sems[s], rank[s][v])
                    if fn is not None:
                        ins = fn(e)
                        if seq in self.needed.get(semname, ()):
                            ins.then_inc(sems[semname], self.incs[semname])

            deco(body)


def _perm_q():
    idx = []
    for j in range(4):
        idx += list(range(j * 64, (j + 1) * 64))
        idx += list(range((4 + j) * 64, (5 + j) * 64))
    return np.array(idx)


def _perm_in_cols():
    pq = _perm_q()
    q0, k0, v0, ga0, gb0, gta0, gtc0 = 0, 512, 640, 768, 1280, 1792, 2816
    cols = []
    cols += list(range(gb0, gb0 + 512))
    cols += list(range(ga0, ga0 + 512))
    cols += list(range(k0, k0 + 128))
    cols += list(range(v0, v0 + 128))
    cols += list(q0 + pq)
    cols += list(range(gta0, gta0 + 1024))
    cols += list(range(gtc0, gtc0 + 1024))
    return np.array(cols)


def _group_layout(W, gsz):
    K, N = W.shape
    KC, M = K // 128, N // 128
    A = W.reshape(KC, 128, M, 128).transpose(1, 2, 0, 3)
    ng = (M + gsz - 1) // gsz
    out = np.zeros((ng, 128, GW), np.float32)
    for g in range(ng):
        m0, m1 = g * gsz, min(M, (g + 1) * gsz)
        blk = A[:, m0:m1].reshape(128, -1)
        out[g, :, :blk.shape[1]] = blk
    return out


def _alibi_table():
    slopes = 2.0 ** (-8.0 * np.arange(1, NQ + 1) / NQ)
    kk = np.arange(128)[:, None]
    qi = np.arange(128)[None, :]
    tab = np.zeros((128, 2, 2, 4, 128), np.float32)
    for g in range(2):
        for c in range(2):
            dist = qi + 128 - c * 128 - kk
            valid = (dist >= 0) & (dist < 128)
            for j in range(4):
                h = 4 * g + j
                tab[:, g, c, j, :] = np.where(valid, -slopes[h] * dist, MASKV)
    return tab.reshape(128, 2048)


def _par_offsets():
    off, o = {}, 0
    for name, n in [("ident", 128), ("g1", L * 8), ("g2", L * 8), ("gf", 8), ("bin", L * 30),
                    ("cw", L * 4 * CONVK), ("cb", L * 4), ("lng", L * 4), ("lnb", L * 4),
                    ("bcp", L * 8), ("flag", 1), ("eps", 1)]:
        off[name] = o
        o += n
    off["_n"] = o
    return off


POFF = _par_offsets()


def _colmajor(v):
    n = v.shape[-1] // 128
    r = v.reshape(v.shape[:-1] + (n, 128))
    return np.moveaxis(r, -1, 0)


def build_program(cfg=None):
    cfg = cfg or {}
    NTILES = cfg.get("ntiles", 9)
    NS = cfg.get("ns", 4)
    dumps = cfg.get("dumps", {})
    nlayers = cfg.get("nlayers", L)
    CONV_OFF = cfg.get("conv_offload", False)
    SPLIT_FIRST = cfg.get("split_first", False)
    PE_CONV = cfg.get("pe_conv", [1, 2, 3])

    nc = bass.Bass("TRN2", target_bir_lowering=False)
    xs = nc.dram_tensor("xs", [SLAB, D], F32, kind="ExternalInput").ap()
    wcat = nc.dram_tensor("wcat", [L * NG * 128, GW], F32, kind="ExternalInput").ap()
    par_d = nc.dram_tensor("par", [128, POFF["_n"]], F32, kind="ExternalInput").ap()
    sk_d = nc.dram_tensor("sk", [128, L * 512], F32, kind="ExternalInput").ap()
    alb_d = nc.dram_tensor("alb", [128, 2048], F32, kind="ExternalInput").ap()
    out_d = nc.dram_tensor("out", [TOK_CORE, D], F32, kind="ExternalOutput").ap()
    wbf = nc.dram_tensor("wbf", [L * NG * 128, GW], BF16, kind="Internal").ap()
    dump_d = {}
    for name, shape in dumps.items():
        dump_d[name] = nc.dram_tensor("dbg_" + name, list(shape), F32, kind="ExternalOutput").ap()

    S = Sched(self_sync=cfg.get("self_sync", False))

    with ExitStack() as es:
        def T(name, shape, dt):
            return es.enter_context(nc.sbuf_tensor("sb_" + name, shape, dt))

        x = T("x", [128, 8, 512], F32)
        xin = T("xin", [128, 2, 1024], F32)
        yout = T("yout", [128, 2, 1024], F32)
        sq = T("sq", [128, 8, 512], BF16)
        h = T("h", [128, 8, 512], BF16)
        rs = T("rs", [128, 512], F32)
        rstd = T("rstd", [128, 512], F32)
        vTf = rs
        qT = T("qT", [128, 4, 512], BF16)
        kT = [T(f"kT{l}", [128, 640], BF16) for l in range(L)]
        vtm = [T(f"vtm{l}", [128, 5, 128], BF16) for l in range(L)]
        sig = T("sig", [128, 4, 512], BF16)
        u = [T(f"u{l}", [128, 4, 544], BF16) for l in range(L)]
        R = T("R", [128, 16384], BF16)
        a_ = R[:, :].rearrange("p (m t) -> p m t", m=32)
        sga = R[:, 0:4096].rearrange("p (m t) -> p m t", m=8)
        sgc = R[:, 4096:8192].rearrange("p (m t) -> p m t", m=8)
        acc = R[:, 8192:12288].bitcast(F32).rearrange("p (m t) -> p m t", m=4)
        ybf = R[:, 12288:14336].rearrange("p (m t) -> p m t", m=4)
        ysq = R[:, 14336:16384].rearrange("p (m t) -> p m t", m=4)
        pe_ = T("pe", [128, 2, 1024], BF16)
        p_ = T("p", [128, 3, 1024], BF16)
        dsum = T("dsum", [128, 1, 512], F32)
        rden = dsum
        attnT = T("attnT", [128, 4, 512], BF16)
        lt = T("lt", [128, 2, 512], F32)
        osb = T("osb", [128, 512], F32)
        lvar = T("lvar", [128, 512], F32)
        musq = lvar
        rl = lvar
        zs = T("zs", [128, 4, 512], BF16)
        t1 = T("t1", [128, 2, 512], F32)
        t2 = T("t2", [128, 2, 512], F32)
        merged = T("merged", [128, 8, 512], BF16)
        dgC = merged[:, :, :].rearrange("p m t -> p (m t)")[:, 0:CONVK * 128].rearrange("p (k c) -> p k c", k=CONVK)
        r_ = T("r", [128, 2, 512], BF16)
        dgA = T("dgA", [128, CONVK, 128], BF16)
        dgB = yout[:, :, :].rearrange("p a b -> p (a b)").bitcast(BF16)[:, 0:CONVK * 128].rearrange("p (k c) -> p k c", k=CONVK)
        dgs = {2: dgA, 3: dgB, 1: dgC}
        dgkeys = {2: ["dgA"], 3: [("yout", s_, h_) for s_ in range(2) for h_ in range(2)], 1: [("merged", m) for m in range(8)]}
        wring = T("wring", [128, NS, GW], BF16)
        par = T("par", [128, POFF["_n"]], F32)
        esf = T("esf", [128, L, 512], F32)
        Eall = T("Eall", [128, 2, 2048], BF16)
        identb = T("identb", [128, 128], BF16)
        lnscr = T("lnscr", [128, 2], F32)
        ones1 = T("ones1", [128, 128], BF16)
        onesD = T("onesD", [128, 128], BF16)
        onesC = T("onesC", [128, 128], BF16)
        ps = es.enter_context(nc.psum_tensor("psum_all", [128, 8, 512], F32))

        semnames = ["pe", "act", "dve", "pool", "sp", "par", "xi0", "xi1", "o0", "o1", "dbg"]
        semnames += [f"w{i}" for i in range(NS)] + [f"c{i}" for i in range(20)]
        sems = {n: es.enter_context(nc.semaphore("sem_" + n)) for n in semnames}

        def P(name, i=0, n=1):
            o = POFF[name] + i
            return par[:, o:o + n]

        ident = par[:, POFF["ident"]:POFF["ident"] + 128]

        cast_spans = [(g, g + 1) for g in range(8)] + [(g, g + 4) for g in range(8, L * NG, 4)]
        cast_of_group = {}
        for ci, (g0, g1) in enumerate(cast_spans):
            for g in range(g0, g1):
                cast_of_group[g] = ci

        def cast_dma(ci):
            g0, g1 = cast_spans[ci]
            S.op("pool", lambda e: e.dma_start(out=wbf[g0 * 128:g1 * 128, :], in_=wcat[g0 * 128:g1 * 128, :],
                                               max_dma_last_dim=8192),
                 writes=[("wbf", ci)], sem=f"c{ci}", inc=16)

        NCAST0 = cfg.get("ncast0", 14)
        for ci in range(NCAST0):
            cast_dma(ci)
        S.op("sp", lambda e: e.dma_start(out=par[:], in_=par_d), writes=["par"], sem="par", inc=16)
        S.op("sp", lambda e: e.dma_start(out=esf[:].rearrange("p a b -> p (a b)"), in_=sk_d), writes=["esf"], sem="par", inc=16)
        xflat = x[:, 0:4, :].rearrange("p a b -> p (a b)")
        S.op("sp", lambda e: e.dma_start(out=xflat, in_=alb_d), writes=[("x", c) for c in range(4)], sem="par", inc=16)
        S.force_token("par", ("par", 3))
        S.force_token("esf", ("par", 3))
        for c in range(4):
            S.force_token(("x", c), ("par", 3))
        S.op("act", lambda e: e.activation(out=esf[:], in_=esf[:], func=AF.Exp), reads=["esf"], writes=["esf"])
        S.op("act", lambda e: e.activation(out=Eall[:, 0, :], in_=xflat, func=AF.Exp),
             reads=[("x", c) for c in range(4)], writes=["E0"])
        E4 = Eall[:, :, :].rearrange("p v (g c n) -> p v g c n", g=2, c=2)
        S.op("dve", lambda e: e.tensor_scalar(out=E4[:, 1, :, 0, :], in0=E4[:, 0, :, 0, :], scalar1=P("flag"), scalar2=None,
                                              op0=ALU.mult), reads=["E0", "par"], writes=["E1a"])
        S.op("dve", lambda e: e.tensor_copy(out=E4[:, 1, :, 1, :], in_=E4[:, 0, :, 1, :]), reads=["E0"], writes=["E1b"])
        S.op("dve", lambda e: e.tensor_copy(out=identb[:], in_=ident), reads=["par"], writes=["identb"])
        S.op("pool", lambda e: e.memset(ones1[:], 1.0), writes=["ones1"])
        S.op("pool", lambda e: e.memset(onesD[:], 1.0 / D), writes=["onesD"])
        S.op("pool", lambda e: e.memset(onesC[:], 1.0 / 512), writes=["onesC"])
        for l in range(L):
            S.op("pool", lambda e, l=l: e.memset(kT[l][:, 0:128], 0.0), writes=[("kT", l)])
            S.op("pool", lambda e, l=l: e.memset(vtm[l][:, 0, :], 0.0), writes=[("vtm", l)])
            S.op("pool", lambda e, l=l: e.memset(u[l][:, :, 0:30], 0.0), writes=[("u", l, j) for j in range(4)])
        EKEYS = ["E0", "E1a", "E1b"]

        def tile_T(ti):
            return 256 if ti == 0 else 512

        def layer_mode(ti, l):
            if ti == 0 and l == nlayers - 1 and nlayers > 1:
                return "kvu"
            return "full"

        wseq = []
        for ti in range(NTILES):
            for l in range(nlayers):
                if layer_mode(ti, l) == "kvu":
                    wseq += [(l, 0, GW), (l, 1, GW), (l, 2, 2048)]
                else:
                    for gi in range(NG):
                        wseq.append((l, gi, 2048 if gi == 7 else GW))

        wst = {"issued": 0, "cons": 0}

        def w_issue():
            n = wst["issued"]
            l, gi, width = wseq[n]
            slot = n % NS
            row = (l * NG + gi) * 128
            S.op("sp", lambda e: e.dma_start(out=wring[:, slot, 0:width], in_=wbf[row:row + 128, 0:width]),
                 reads=[("wbf", cast_of_group[l * NG + gi])], writes=[("w", slot)], sem=f"w{slot}", inc=16)
            wst["issued"] += 1

        def w_slot(l, gi, ofs=0):
            n = wst["cons"] + ofs
            assert wseq[n][:2] == (l, gi), (wseq[n], l, gi)
            assert n < wst["issued"]
            return n % NS

        def w_release():
            wst["cons"] += 1
            if wst["issued"] < len(wseq):
                w_issue()

        for _ in range(min(NS, len(wseq))):
            w_issue()

        xblocks = []
        for ti in range(NTILES):
            row0 = 0 if ti == 0 else HALO + (ti - 1) * 512
            for b in range(tile_T(ti) // 128):
                xblocks.append(row0 + b * 128)
        xst = {"issued": 0, "cons": 0}

        def x_issue():
            n = xst["issued"]
            slot = n % 2
            row = xblocks[n]
            S.op("sp", lambda e: e.dma_start(out=xin[:, slot, :], in_=xs[row:row + 128, :]),
                 writes=[("xin", slot)], sem=f"xi{slot}", inc=16)
            xst["issued"] += 1

        for _ in range(2):
            x_issue()

        mmctr = [0]

        def mmbank():
            b = mmctr[0] % 3
            mmctr[0] += 1
            return b

        def mm_group(out_ap, pairs):
            def fn(e):
                n = len(pairs)
                ins = None
                for i, (a, b) in enumerate(pairs):
                    ins = e.matmul(out_ap, lhsT=a, rhs=b, start=(i == 0), stop=(i == n - 1))
                return ins
            return fn

        def mm_chain(bank, Tt, wslot, wofs, act, akeys, wkey):
            n = len(akeys)
            pairs = [(wring[:, wslot, (wofs + kc) * 128:(wofs + kc + 1) * 128], act[:, kc, 0:Tt]) for kc in range(n)]
            if firstk[0] and SPLIT_FIRST:
                firstk[0] = False
                for kc in range(n):
                    a_, b_ = pairs[kc]
                    S.op("pe", lambda e, a_=a_, b_=b_, kc=kc: e.matmul(ps[:, bank, 0:Tt], lhsT=a_, rhs=b_,
                                                                      start=(kc == 0), stop=(kc == n - 1)),
                         reads=[akeys[kc], wkey], writes=[("ps", bank)] if kc == n - 1 else [("psx", bank)] if False else [("ps", bank)])
            else:
                S.op("pe", mm_group(ps[:, bank, 0:Tt], pairs), reads=list(akeys) + [wkey], writes=[("ps", bank)])

        def kmajor_first(Tt, wslot, mis, act, akeys, wkey):
            banks = {mi: mmbank() for mi in mis}
            n = len(akeys)
            for kc in range(n):
                def fn(e, kc=kc):
                    ins = None
                    for mi in mis:
                        ins = e.matmul(ps[:, banks[mi], 0:Tt], lhsT=wring[:, wslot, (mi * n + kc) * 128:(mi * n + kc + 1) * 128],
                                       rhs=act[:, kc, 0:Tt], start=(kc == 0), stop=(kc == n - 1))
                    return ins
                S.op("pe", fn, reads=[akeys[kc], wkey], writes=[("ps", banks[mi]) for mi in mis])
            return banks

        KMAJOR = cfg.get("kmajor", True)
        BLOCK_STATS = cfg.get("block_stats", True)
        evctr = [0]
        dbgctr = [0]

        def dump(name, ap, keys):
            if name not in dump_d:
                return
            dst = dump_d.pop(name)
            S.op("pool", lambda e: e.dma_start(out=dst, in_=ap), reads=keys, sem="dbg", inc=16)
            dbgctr[0] += 16

        pre_stats = [False]
        pend = []
        STAT_LAG = cfg.get("stat_lag", 3)

        def stat_emit(m, Tt):
            S.op("pe", lambda e: e.matmul(ps[:, 3, 0:Tt], lhsT=onesD[:], rhs=sq[:, m, 0:Tt], start=(m == 0), stop=(m == 7)),
                 reads=[("sq", m), "onesD"], writes=[("ps", 3)])

        def resid_add(m, bank, Tt):
            S.op("dve", lambda e: e.tensor_tensor(
                out=x[:, m, 0:Tt], in0=x[:, m, 0:Tt], in1=ps[:, bank, 0:Tt], op=ALU.add),
                reads=[("ps", bank), ("x", m)], writes=[("x", m)])
            S.op("act", lambda e: e.activation(out=sq[:, m, 0:Tt], in_=x[:, m, 0:Tt], func=AF.Square),
                 reads=[("x", m)], writes=[("sq", m)])
            pend.append((m, Tt))
            while len(pend) > STAT_LAG:
                stat_emit(*pend.pop(0))
            if m == 7:
                while pend:
                    stat_emit(*pend.pop(0))
                pre_stats[0] = True

        def norm(Tt, gname, l, tag, pre=False):
            xk = [("x", c) for c in range(8)]
            sqk = [("sq", c) for c in range(8)]
            if not pre:
                S.op("act", lambda e: e.activation(out=sq[:, :, 0:Tt], in_=x[:, :, 0:Tt], func=AF.Square), reads=xk, writes=sqk)
                S.op("pe", mm_group(ps[:, 3, 0:Tt], [(onesD[:], sq[:, c, 0:Tt]) for c in range(8)]),
                     reads=sqk + ["onesD"], writes=[("ps", 3)])
            S.op("act", lambda e: e.activation(out=rs[:, 0:Tt], in_=ps[:, 3, 0:Tt], func=AF.Ln, bias=P("eps"), scale=1.0),
                 reads=[("ps", 3), "par"], writes=["rs"])
            S.op("act", lambda e: e.activation(out=rstd[:, 0:Tt], in_=rs[:, 0:Tt], func=AF.Exp, scale=-0.5),
                 reads=["rs"], writes=["rstd"])

        def norm_h(Tt, gname, l, pre=False):
            norm(Tt, gname, l, "h", pre=pre)
            if gname == "g1":
                S.op("act", lambda e: e.activation(out=lnscr[:, 1:2], in_=P("eps"), func=AF.Sigmoid), reads=["par"], writes=["lnscr2"])
            for c in range(8):
                gi = (l * 8 + c) if gname != "gf" else c
                S.op("dve", lambda e, c=c, gi=gi: e.scalar_tensor_tensor(
                    out=h[:, c, 0:Tt], in0=x[:, c, 0:Tt], scalar=P(gname, gi), in1=rstd[:, 0:Tt],
                    op0=ALU.mult, op1=ALU.mult), reads=[("x", c), "rstd", "par"], writes=[("h", c)])
            firstk[0] = True

        firstk = [False]
        RA = [("R", "A")]
        RB = [("R", "B")]
        blkctr = [0]
        outctr = [0]
        bufctr = [0]

        def do_tile(ti):
            Tt = tile_T(ti)
            NBt = Tt // 128
            bst_pending = []
            sq_pending = {"act": None, "dve": None}
            for b in range(NBt):
                n = xst["cons"]
                slot = n % 2
                for half in range(2):
                    bank = mmbank()
                    def tr(e, slot=slot, half=half, bank=bank):
                        ins = None
                        for cc in range(4):
                            c = half * 4 + cc
                            ins = e.transpose(ps[:, bank, cc * 128:(cc + 1) * 128], xin[:, slot, c * 128:(c + 1) * 128], ident)
                        return ins
                    S.op("pe", tr, reads=[("xin", slot), "par"], writes=[("ps", bank)])
                    dst = x[:, half * 4:half * 4 + 4, b * 128:(b + 1) * 128]
                    src = ps[:, bank, :].rearrange("p (c t) -> p c t", c=4)
                    wk = [("x", half * 4 + cc) for cc in range(4)]
                    sqd = sq[:, half * 4:half * 4 + 4, b * 128:(b + 1) * 128]
                    sqk = [("sq", half * 4 + cc) for cc in range(4)]
                    if half == 0:
                        S.op("act", lambda e, dst=dst, src=src: e.activation(out=dst, in_=src, func=AF.Copy),
                             reads=[("ps", bank)], writes=wk)
                        if BLOCK_STATS:
                            if sq_pending["act"]:
                                sq_pending["act"]()
                            sq_pending["act"] = (lambda dst=dst, sqd=sqd, wk=wk, sqk=sqk: S.op(
                                "act", lambda e: e.activation(out=sqd, in_=dst, func=AF.Square), reads=wk, writes=sqk))
                    else:
                        S.op("dve", lambda e, dst=dst, src=src: e.tensor_copy(out=dst, in_=src),
                             reads=[("ps", bank)], writes=wk)
                        if BLOCK_STATS:
                            if sq_pending["dve"]:
                                sq_pending["dve"]()
                            sq_pending["dve"] = (lambda dst=dst, sqd=sqd, wk=wk, sqk=sqk: S.op(
                                "dve", lambda e: e.tensor_tensor(out=sqd, in0=dst, in1=dst, op=ALU.mult), reads=wk, writes=sqk))
                xst["cons"] += 1
                if xst["issued"] < len(xblocks):
                    x_issue()
                if BLOCK_STATS:
                    bc = slice(b * 128, (b + 1) * 128)
                    xk8 = [("x", c) for c in range(8)]
                    sqk8 = [("sq", c) for c in range(8)]
                    def bst(e, bc=bc):
                        ins = None
                        for c in range(8):
                            ins = e.matmul(ps[:, 3, bc], lhsT=onesD[:], rhs=sq[:, c, bc], start=(c == 0), stop=(c == 7))
                        return ins
                    if len(bst_pending) >= 2:
                        S.op("pe", bst_pending.pop(0), reads=sqk8 + ["onesD"], writes=[("ps", 3)])
                    bst_pending.append(bst)
            if BLOCK_STATS:
                for en in ("act", "dve"):
                    if sq_pending[en]:
                        sq_pending[en]()
                        sq_pending[en] = None
                while bst_pending:
                    S.op("pe", bst_pending.pop(0), reads=[("sq", c) for c in range(8)] + ["onesD"], writes=[("ps", 3)])
                pre_stats[0] = True
            if ti == 0:
                dump("x0", x[:, :, 0:256], [("x", c) for c in range(8)])

            def do_layer(l):
                mode = layer_mode(ti, l)
                full = mode == "full"
                norm_h(Tt, "g1", l, pre=pre_stats[0])
                pre_stats[0] = False
                pending_builds = []
                if full:
                    for j in PE_CONV:
                        for k in range(CONVK):
                            pending_builds.append((j, k))

                def emit_builds(n):
                    for _ in range(min(n, len(pending_builds))):
                        j, k = pending_builds.pop(0)
                        S.op("dve", lambda e, j=j, k=k: e.tensor_scalar(
                            out=dgs[j][:, k, :], in0=identb[:], scalar1=P("cw", (l * 4 + j) * CONVK + k), scalar2=None,
                            op0=ALU.mult),
                            reads=["identb", "par"], writes=[("dg", j, k)] + (dgkeys[j] if k == 0 else []), regions=())

                emit_builds(16)
                if ti == 0 and l == 0:
                    dump("h0", h[:, :, 0:256], [("h", c) for c in range(8)])
                hk = [("h", c) for c in range(8)]
                mlist = list(range(30)) if full else list(range(10))
                ngroups = 8 if full else 3

                def gen_in():
                    for gi in range(ngroups):
                        slot = w_slot(l, gi)
                        pre = {}
                        if firstk[0] and KMAJOR:
                            firstk[0] = False
                            pre = kmajor_first(Tt, slot, [0, 1, 2], h, hk, ("w", slot))
                        for mi in range(4):
                            m = gi * 4 + mi
                            if m not in mlist:
                                continue
                            if mi in pre:
                                bank = pre[mi]
                            else:
                                bank = mmbank()
                                mm_chain(bank, Tt, slot, mi * 8, h, hk, ("w", slot))
                            src = ps[:, bank, 0:Tt]
                            bias = P("bin", l * 30 + m)
                            if m < 4:
                                S.op("act", lambda e, src=src, bias=bias, m=m: e.activation(
                                    out=sig[:, m, 0:Tt], in_=src, func=AF.Sigmoid, bias=bias, scale=1.0),
                                    reads=[("ps", bank), "par"], writes=[("sig", m)])
                            elif m < 8:
                                j = m - 4
                                S.op("dve", lambda e, src=src, bias=bias, j=j: e.scalar_tensor_tensor(
                                    out=u[l][:, j, 30:30 + Tt], in0=src, scalar=bias, in1=sig[:, j, 0:Tt],
                                    op0=ALU.add, op1=ALU.mult),
                                    reads=[("ps", bank), "par", ("sig", j)], writes=[("u", l, j)])
                            elif m == 8:
                                S.op("act", lambda e, src=src, bias=bias: e.activation(
                                    out=kT[l][:, 128:128 + Tt], in_=src, func=AF.Identity, bias=bias, scale=1.0),
                                    reads=[("ps", bank), "par"], writes=[("kT", l)])
                            elif m == 9:
                                S.op("act", lambda e, src=src, bias=bias: e.activation(
                                    out=vTf[:, 0:Tt], in_=src, func=AF.Identity, bias=bias, scale=1.0),
                                    reads=[("ps", bank), "par"], writes=["vTf"])
                                vb = mmbank()
                                def vtr(e, vb=vb):
                                    ins = None
                                    for b in range(NBt):
                                        ins = e.transpose(ps[:, vb, b * 128:(b + 1) * 128], vTf[:, b * 128:(b + 1) * 128], ident)
                                    return ins
                                S.op("pe", vtr, reads=["vTf", "par"], writes=[("ps", vb)])
                                S.op("dve", lambda e, vb=vb: e.tensor_copy(
                                    out=vtm[l][:, 1:1 + NBt, :], in_=ps[:, vb, 0:Tt].rearrange("p (b f) -> p b f", b=NBt)),
                                    reads=[("ps", vb)], writes=[("vtm", l)])
                            elif m < 14:
                                j = m - 10
                                S.op("act", lambda e, src=src, bias=bias, j=j: e.activation(
                                    out=qT[:, j, 0:Tt], in_=src, func=AF.Identity, bias=bias, scale=1.0),
                                    reads=[("ps", bank), "par"], writes=[("qT", j)])
                            else:
                                gt, i = (sga, m - 14) if m < 22 else (sgc, m - 22)
                                gk = ("sga", i) if m < 22 else ("sgc", i)
                                if m % 2 == 0:
                                    S.op("act", lambda e, src=src, bias=bias, gt=gt, i=i: e.activation(
                                        out=gt[:, i, 0:Tt], in_=src, func=AF.Identity, bias=bias, scale=1.0),
                                        reads=[("ps", bank), "par"], writes=[gk], regions=RA)
                                else:
                                    S.op("dve", lambda e, src=src, bias=bias, gt=gt, i=i: e.tensor_scalar(
                                        out=gt[:, i, 0:Tt], in0=src, scalar1=bias, scalar2=None, op0=ALU.add),
                                        reads=[("ps", bank), "par"], writes=[gk], regions=RA)
                            if mi == 3 or m == mlist[-1]:
                                w_release()
                            emit_builds(11)
                            yield

                def att_front(nb, g, bf, pb):
                    gs = slice(g * 64, (g + 1) * 64)
                    var = 1 if (ti == 1 and nb == 0) else 0
                    qk = [("qT", j) for j in range(4)]
                    def smm(e):
                        e.matmul(ps[:, 4, :].rearrange("p (j q) -> p j q", j=4), lhsT=kT[l][gs, nb * 128:(nb + 1) * 128],
                                 rhs=qT[gs, :, nb * 128:(nb + 1) * 128], start=True, stop=True)
                        return e.matmul(ps[:, 5, :].rearrange("p (j q) -> p j q", j=4),
                                        lhsT=kT[l][gs, (nb + 1) * 128:(nb + 2) * 128],
                                        rhs=qT[gs, :, nb * 128:(nb + 1) * 128], start=True, stop=True)
                    S.op("pe", smm, reads=qk + [("kT", l)], writes=[("ps", 4), ("ps", 5)])
                    S.op("act", lambda e: e.activation(
                        out=pe_[:, pb, :], in_=ps[:, 4:6, :].rearrange("p a b -> p (a b)"), func=AF.Exp, scale=0.125),
                        reads=[("ps", 4), ("ps", 5)], writes=[("pe", pb)])
                    S.op("dve", lambda e: e.tensor_tensor(
                        out=p_[:, bf, :], in0=pe_[:, pb, :], in1=Eall[:, var, g * 1024:(g + 1) * 1024], op=ALU.mult),
                        reads=[("pe", pb)] + EKEYS, writes=[("p", bf)])

                def att_pv(nb, g, bf):
                    gs = slice(g * 64, (g + 1) * 64)
                    def pv(e):
                        e.matmul(ps[gs, 6, :], lhsT=vtm[l][:, nb, gs], rhs=p_[:, bf, 0:512], start=True, stop=False)
                        e.matmul(ps[gs, 6, :], lhsT=vtm[l][:, nb + 1, gs], rhs=p_[:, bf, 512:1024], start=False, stop=True)
                        e.matmul(ps[gs, 7, :], lhsT=ones1[:, 0:64], rhs=p_[:, bf, 0:512], start=True, stop=False)
                        return e.matmul(ps[gs, 7, :], lhsT=ones1[:, 0:64], rhs=p_[:, bf, 512:1024], start=False, stop=True)
                    S.op("pe", pv, reads=[("p", bf), ("vtm", l), "ones1"], writes=[("ps", 6, g), ("ps", 7, g)])

                def att_norm(nb, db):
                    S.op("dve", lambda e: e.tensor_tensor(
                        out=dsum[:, db, :], in0=ps[:, 7, :], in1=esf[:, l, :], op=ALU.add),
                        reads=[("ps", 7, 0), ("ps", 7, 1), "esf"], writes=[("dsum", db)])
                    S.op("act", lambda e: e.activation(out=osb[:], in_=ps[:, 6, :], func=AF.Copy),
                         reads=[("ps", 6, 0), ("ps", 6, 1)], writes=["osb"])
                    S.op("act", lambda e: e.activation(out=dsum[:, db, :], in_=dsum[:, db, :], func=AF.Ln),
                         reads=[("dsum", db)], writes=[("dsum", db)])
                    S.op("act", lambda e: e.activation(out=rden[:, db, :], in_=dsum[:, db, :], func=AF.Exp, scale=-1.0),
                         reads=[("dsum", db)], writes=[("dsum", db)])
                    S.op("dve", lambda e: e.tensor_tensor(
                        out=attnT[:, :, nb * 128:(nb + 1) * 128],
                        in0=osb[:].rearrange("p (j q) -> p j q", j=4),
                        in1=rden[:, db, :].rearrange("p (j q) -> p j q", j=4), op=ALU.mult),
                        reads=["osb", ("dsum", db)], writes=[("attnT", 0), ("attnT", 1)])

                def gen_att():
                    steps = [(nb, g) for nb in range(NBt) for g in range(2)]
                    bfs = []
                    for _ in steps:
                        bfs.append(bufctr[0] % 3)
                        bufctr[0] += 1
                    att_front(steps[0][0], steps[0][1], bfs[0], 0)
                    yield
                    att_front(steps[1][0], steps[1][1], bfs[1], 1)
                    yield
                    for i, (nb, g) in enumerate(steps):
                        if i + 2 < len(steps):
                            att_front(steps[i + 2][0], steps[i + 2][1], bfs[i + 2], i % 2)
                        att_pv(nb, g, bfs[i])
                        if g == 1:
                            att_norm(nb, 0)
                        yield

                def gen_conv():
                    for k in range(CONVK):
                        for j in range(4):
                            if j in PE_CONV:
                                continue
                            wk_ = P("cw", (l * 4 + j) * CONVK + k)
                            cb_ = P("cb", l * 4 + j)
                            if j < 2 or not CONV_OFF:
                                if k == 0:
                                    S.op("dve", lambda e, j=j, wk_=wk_, cb_=cb_: e.tensor_scalar(
                                        out=acc[:, j, 0:Tt], in0=u[l][:, j, 0:Tt], scalar1=wk_, scalar2=cb_,
                                        op0=ALU.mult, op1=ALU.add),
                                        reads=[("u", l, j), "par"], writes=[("acc", j)], regions=RA)
                                else:
                                    S.op("dve", lambda e, j=j, k=k, wk_=wk_: e.scalar_tensor_tensor(
                                        out=acc[:, j, 0:Tt], in0=u[l][:, j, k:k + Tt], scalar=wk_, in1=acc[:, j, 0:Tt],
                                        op0=ALU.mult, op1=ALU.add),
                                        reads=[("u", l, j), ("acc", j), "par"], writes=[("acc", j)], regions=RA)
                            else:
                                if k == 0:
                                    S.op("act", lambda e, j=j, wk_=wk_, cb_=cb_: e.activation(
                                        out=acc[:, j, 0:Tt], in_=u[l][:, j, 0:Tt], func=AF.Identity, scale=wk_, bias=cb_),
                                        reads=[("u", l, j), "par"], writes=[("acc", j)], regions=RA)
                                else:
                                    sl = (j - 2) * 2 + (k % 2)
                                    S.op("act", lambda e, j=j, k=k, wk_=wk_, sl=sl: e.activation(
                                        out=tmpc[:, sl, 0:Tt], in_=u[l][:, j, k:k + Tt], func=AF.Identity, scale=wk_),
                                        reads=[("u", l, j), "par"], writes=[("tmpc", sl)])
                                    S.op("pool", lambda e, j=j, sl=sl: e.tensor_tensor(
                                        out=acc[:, j, 0:Tt], in0=acc[:, j, 0:Tt], in1=tmpc[:, sl, 0:Tt], op=ALU.add),
                                        reads=[("tmpc", sl), ("acc", j)], writes=[("acc", j)], regions=RA)
                        yield

                def step(g):
                    try:
                        next(g)
                        return g
                    except StopIteration:
                        return None

                def gen_ln():
                    acck = [("acc", j) for j in range(4)]
                    dvej = [j for j in range(4) if j not in PE_CONV]
                    pej = [j for j in range(4) if j in PE_CONV]
                    for j in dvej:
                        S.op("act", lambda e, j=j: e.activation(out=ybf[:, j, 0:Tt], in_=acc[:, j, 0:Tt], func=AF.Copy),
                             reads=[("acc", j)], writes=[("ybf", j)], regions=RA)
                        S.op("act", lambda e, j=j: e.activation(out=ysq[:, j, 0:Tt], in_=acc[:, j, 0:Tt], func=AF.Square),
                             reads=[("acc", j)], writes=[("ysq", j)], regions=RA)
                    yield
                    m2b = mmbank()
                    order = pej + dvej
                    def ln_mm(e, m2b=m2b, js=None, first=True, last=True):
                        ins = None
                        for i, j in enumerate(js):
                            ins = e.matmul(ps[:, 3, 0:Tt], lhsT=onesC[:], rhs=ybf[:, j, 0:Tt], start=(first and i == 0),
                                           stop=(last and i == len(js) - 1))
                        for i, j in enumerate(js):
                            ins = e.matmul(ps[:, m2b, 0:Tt], lhsT=onesC[:], rhs=ysq[:, j, 0:Tt], start=(first and i == 0),
                                           stop=(last and i == len(js) - 1))
                        return ins
                    if pej and dvej:
                        S.op("pe", lambda e: ln_mm(e, js=pej, first=True, last=False),
                             reads=[("ybf", j) for j in pej] + [("ysq", j) for j in pej] + ["onesC"],
                             writes=[("ps", 3), ("ps", m2b)], regions=RA)
                        S.op("pe", lambda e: ln_mm(e, js=dvej, first=False, last=True),
                             reads=[("ybf", j) for j in dvej] + [("ysq", j) for j in dvej] + ["onesC"],
                             writes=[("ps", 3), ("ps", m2b)], regions=RA)
                    else:
                        S.op("pe", lambda e: ln_mm(e, js=order, first=True, last=True),
                             reads=[("ybf", j) for j in order] + [("ysq", j) for j in order] + ["onesC"],
                             writes=[("ps", 3), ("ps", m2b)], regions=RA)
                    S.op("act", lambda e: e.activation(out=musq[:, 0:Tt], in_=ps[:, 3, 0:Tt], func=AF.Square),
                         reads=[("ps", 3)], writes=["lvar"])
                    S.op("dve", lambda e, m2b=m2b: e.scalar_tensor_tensor(out=lvar[:, 0:Tt], in0=ps[:, m2b, 0:Tt], scalar=EPS,
                                                                         in1=musq[:, 0:Tt], op0=ALU.add, op1=ALU.subtract),
                         reads=[("ps", m2b), "lvar"], writes=["lvar"])
                    yield
                    S.op("act", lambda e: e.activation(out=lvar[:, 0:Tt], in_=lvar[:, 0:Tt], func=AF.Ln),
                         reads=["lvar"], writes=["lvar"])
                    S.op("act", lambda e: e.activation(out=rl[:, 0:Tt], in_=lvar[:, 0:Tt], func=AF.Exp, scale=-0.5),
                         reads=["lvar"], writes=["lvar"])
                    yield
                    for j in range(4):
                        if j == 2:
                            yield
                        bf = j % 2
                        S.op("dve", lambda e, j=j, bf=bf: e.tensor_tensor(out=lt[:, bf, 0:Tt], in0=acc[:, j, 0:Tt], in1=ps[:, 3, 0:Tt],
                                                                          op=ALU.subtract),
                             reads=[("acc", j), ("ps", 3)], writes=[("lt", bf)], regions=RA)
                        S.op("dve", lambda e, bf=bf: e.tensor_tensor(out=lt[:, bf, 0:Tt], in0=lt[:, bf, 0:Tt], in1=rl[:, 0:Tt],
                                                                     op=ALU.mult),
                             reads=[("lt", bf), "lvar"], writes=[("lt", bf)])
                        S.op("act", lambda e, j=j, bf=bf: e.activation(
                            out=zs[:, j, 0:Tt], in_=lt[:, bf, 0:Tt], func=AF.Silu,
                            bias=P("lnb", l * 4 + j), scale=P("lng", l * 4 + j)),
                            reads=[("lt", bf), "par"], writes=[("zs", j)])

                gin = gen_in()
                n_in = 0
                for _ in range(8):
                    gin = step(gin)
                    n_in += 1
                gcv = gen_conv() if full else None
                gat = None
                att_started = False
                pe_conv_done = not (full and PE_CONV)
                ln_done = False
                gln = None

                def pe_conv():
                    for j in PE_CONV:
                        bank = mmbank()
                        S.op("pe", mm_group(ps[:, bank, 0:Tt], [(dgs[j][:, k, :], u[l][:, j, k:k + Tt]) for k in range(CONVK)]),
                             reads=[("u", l, j)] + [("dg", j, k) for k in range(CONVK)] + dgkeys[j], writes=[("ps", bank)],
                             regions=())
                        S.op("act", lambda e, j=j, bank=bank: e.activation(
                            out=acc[:, j, 0:Tt], in_=ps[:, bank, 0:Tt], func=AF.Identity, bias=P("cb", l * 4 + j), scale=1.0),
                            reads=[("ps", bank), "par"], writes=[("acc", j)], regions=RA)
                        S.op("act", lambda e, j=j, bank=bank: e.activation(
                            out=ybf[:, j, 0:Tt], in_=ps[:, bank, 0:Tt], func=AF.Identity, bias=P("cb", l * 4 + j), scale=1.0),
                            reads=[("ps", bank), "par"], writes=[("ybf", j)], regions=RA)
                        S.op("act", lambda e, j=j, bank=bank: e.activation(
                            out=ysq[:, j, 0:Tt], in_=ps[:, bank, 0:Tt], func=AF.Square, bias=P("cb", l * 4 + j), scale=1.0),
                            reads=[("ps", bank), "par"], writes=[("ysq", j)], regions=RA)

                while gin or gcv or gat or gln:
                    if gin:
                        gin = step(gin)
                        n_in += 1
                    if not pe_conv_done and n_in >= cfg.get("pe_conv_at", 12):
                        emit_builds(1000)
                        pe_conv()
                        pe_conv_done = True
                    if full and not att_started and n_in >= 14:
                        gat = gen_att()
                        att_started = True
                    if gat:
                        gat = step(gat)
                    if gcv:
                        gcv = step(gcv)
                    if gcv:
                        gcv = step(gcv)
                    if full and not ln_done and gcv is None and pe_conv_done:
                        gln = gen_ln()
                        ln_done = True
                    if gln:
                        gln = step(gln)
                if ti == 0 and l == 0:
                    dump("q0", qT[:, :, 0:256], [("qT", j) for j in range(4)])
                    dump("k0", kT[0][:, 0:384], [("kT", 0)])
                    dump("v0", vtm[0][:, 0:3, :], [("vtm", 0)])
                    dump("u0", u[0][:, :, 0:286], [("u", 0, j) for j in range(4)])
                    if full:
                        dump("attn0", attnT[:, :, 0:256], [("attnT", 0), ("attnT", 1)])
                S.op("pool", lambda e: e.tensor_copy(out=kT[l][:, 0:128], in_=kT[l][:, Tt:Tt + 128]),
                     reads=[("kT", l)], writes=[("kT", l)])
                S.op("pool", lambda e: e.tensor_copy(out=vtm[l][:, 0, :], in_=vtm[l][:, NBt, :]),
                     reads=[("vtm", l)], writes=[("vtm", l)])
                uk = [("u", l, j) for j in range(4)]
                if ti == 0:
                    S.op("pool", lambda e: e.tensor_scalar(
                        out=u[l][:, :, 0:30], in0=u[l][:, :, Tt:Tt + 30], scalar1=P("flag"), scalar2=None, op0=ALU.mult),
                        reads=uk + ["par"], writes=uk)
                else:
                    S.op("pool", lambda e: e.tensor_copy(out=u[l][:, :, 0:30], in_=u[l][:, :, Tt:Tt + 30]),
                         reads=uk, writes=uk)

                if not full:
                    return
                if not ln_done:
                    gln = gen_ln()
                while gln:
                    gln = step(gln)
                if ti == 0 and l == 0:
                    dump("acc0", acc[:, :, 0:256], [("acc", j) for j in range(4)])
                if ti == 0 and l == 0:
                    dump("zs0", zs[:, :, 0:256], [("zs", j) for j in range(4)])

                for m in range(2):
                    S.op("act", lambda e, m=m: e.activation(out=sga[:, m, 0:Tt], in_=sga[:, m, 0:Tt], func=AF.Sigmoid),
                         reads=[("sga", m)], writes=[("sga", m)], regions=RA)
                    S.op("act", lambda e, m=m: e.activation(out=sgc[:, m, 0:Tt], in_=sgc[:, m, 0:Tt], func=AF.Sigmoid),
                         reads=[("sgc", m)], writes=[("sgc", m)], regions=RA)
                sA = w_slot(l, 8, 0)
                sC = w_slot(l, 9, 1)
                for m in range(8):
                    bA = mmbank()
                    S.op("pe", mm_group(ps[:, bA, 0:Tt],
                                        [(wring[:, sA, (m * 4 + c) * 128:(m * 4 + c + 1) * 128], attnT[:, c, 0:Tt]) for c in range(4)]),
                         reads=[("attnT", 0), ("attnT", 1), ("w", sA)], writes=[("ps", bA)])
                    bC = mmbank()
                    S.op("pe", mm_group(ps[:, bC, 0:Tt],
                                        [(wring[:, sC, (m * 4 + c) * 128:(m * 4 + c + 1) * 128], zs[:, c, 0:Tt]) for c in range(4)]),
                         reads=[("zs", j) for j in range(4)] + [("w", sC)], writes=[("ps", bC)])
                    bf = m % 2
                    if m + 2 < 8:
                        S.op("act", lambda e, m=m: e.activation(out=sga[:, m + 2, 0:Tt], in_=sga[:, m + 2, 0:Tt], func=AF.Sigmoid),
                             reads=[("sga", m + 2)], writes=[("sga", m + 2)], regions=RA)
                        S.op("act", lambda e, m=m: e.activation(out=sgc[:, m + 2, 0:Tt], in_=sgc[:, m + 2, 0:Tt], func=AF.Sigmoid),
                             reads=[("sgc", m + 2)], writes=[("sgc", m + 2)], regions=RA)
                    else:
                        if m == 6:
                            S.op("act", lambda e: e.activation(out=lnscr[:, 0:1], in_=P("eps"), func=AF.Ln),
                                 reads=["par"], writes=["lnscr"])
                    S.op("dve", lambda e, bA=bA, bf=bf, m=m: e.tensor_tensor(
                        out=t1[:, bf, 0:Tt], in0=ps[:, bA, 0:Tt], in1=sga[:, m, 0:Tt], op=ALU.mult),
                        reads=[("ps", bA), ("sga", m)], writes=[("t1", bf)], regions=RA)
                    S.op("dve", lambda e, bC=bC, bf=bf, m=m: e.scalar_tensor_tensor(
                        out=t2[:, bf, 0:Tt], in0=ps[:, bC, 0:Tt], scalar=P("bcp", l * 8 + m), in1=sgc[:, m, 0:Tt],
                        op0=ALU.add, op1=ALU.mult),
                        reads=[("ps", bC), ("sgc", m), "par"], writes=[("t2", bf)], regions=RA)
                    S.op("pool", lambda e, bf=bf, m=m: e.tensor_tensor(
                        out=merged[:, m, 0:Tt], in0=t1[:, bf, 0:Tt], in1=t2[:, bf, 0:Tt], op=ALU.add),
                        reads=[("t1", bf), ("t2", bf)], writes=[("merged", m)])
                w_release()
                w_release()
                if ti == 0 and l == 0:
                    dump("sga0", sga[:, :, 0:256], [("sga", i) for i in range(8)])
                    dump("merged0", merged[:, :, 0:256], [("merged", m) for m in range(8)])

                mk = [("merged", m) for m in range(8)]
                for gi in (10, 11):
                    slot = w_slot(l, gi)
                    pre = {}
                    if gi == 10 and KMAJOR:
                        pre = kmajor_first(Tt, slot, [0, 1, 2], merged, mk, ("w", slot))
                    for mi in range(4):
                        m = (gi - 10) * 4 + mi
                        if mi in pre:
                            bank = pre[mi]
                        else:
                            bank = mmbank()
                            S.op("pe", mm_group(ps[:, bank, 0:Tt],
                                                [(wring[:, slot, (mi * 8 + c) * 128:(mi * 8 + c + 1) * 128], merged[:, c, 0:Tt])
                                                 for c in range(8)]),
                                 reads=mk + [("w", slot)], writes=[("ps", bank)])
                        resid_add(m, bank, Tt)
                    w_release()
                if ti == 0 and l == 0:
                    dump("x1", x[:, :, 0:256], [("x", c) for c in range(8)])

                norm_h(Tt, "g2", l, pre=pre_stats[0])
                pre_stats[0] = False
                for gi in range(12, 20):
                    slot = w_slot(l, gi)
                    pre = {}
                    if firstk[0] and KMAJOR:
                        firstk[0] = False
                        pre = kmajor_first(Tt, slot, [0, 1, 2], h, hk, ("w", slot))
                    for mi in range(4):
                        m = (gi - 12) * 4 + mi
                        if mi in pre:
                            bank = pre[mi]
                        else:
                            bank = mmbank()
                            mm_chain(bank, Tt, slot, mi * 8, h, hk, ("w", slot))
                        bf = m % 2
                        S.op("act", lambda e, bank=bank, bf=bf: e.activation(out=r_[:, bf, 0:Tt], in_=ps[:, bank, 0:Tt], func=AF.Relu),
                             reads=[("ps", bank)], writes=[("r", bf)])
                        S.op("pool", lambda e, bf=bf, m=m: e.tensor_tensor(
                            out=a_[:, m, 0:Tt], in0=r_[:, bf, 0:Tt], in1=r_[:, bf, 0:Tt], op=ALU.mult),
                            reads=[("r", bf)], writes=[("a", m)], regions=RB)
                    w_release()
                ak = [("a", m) for m in range(32)]
                for gi in range(20, 28):
                    slot = w_slot(l, gi)
                    m = gi - 20
                    bank = mmbank()
                    if gi == 20:
                        def part(e, c0, c1, bank=bank, slot=slot):
                            ins = None
                            for c in range(c0, c1):
                                ins = e.matmul(ps[:, bank, 0:Tt], lhsT=wring[:, slot, c * 128:(c + 1) * 128], rhs=a_[:, c, 0:Tt],
                                               start=(c == 0), stop=(c == 31))
                            return ins
                        S.op("pe", lambda e, part=part: part(e, 0, 26), reads=ak[0:26] + [("w", slot)], writes=[("ps", bank)],
                             regions=RB)
                        S.op("pe", lambda e, part=part: part(e, 26, 32), reads=ak[26:32] + [("w", slot)], writes=[("ps", bank)],
                             regions=RB)
                    else:
                        S.op("pe", mm_group(ps[:, bank, 0:Tt],
                                            [(wring[:, slot, c * 128:(c + 1) * 128], a_[:, c, 0:Tt]) for c in range(32)]),
                             reads=ak + [("w", slot)], writes=[("ps", bank)], regions=RB)
                    resid_add(m, bank, Tt)
                    w_release()
                if ti == 0 and l == 0:
                    dump("x2", x[:, :, 0:256], [("x", c) for c in range(8)])

            for l in range(nlayers):
                do_layer(l)
                if ti == 0 and l == 0:
                    for ci in range(NCAST0, len(cast_spans)):
                        cast_dma(ci)
            if ti == 0:
                return
            norm(Tt, "gf", 0, "f", pre=pre_stats[0])
            pre_stats[0] = False
            for c in range(8):
                S.op("dve", lambda e, c=c: e.scalar_tensor_tensor(
                    out=x[:, c, 0:Tt], in0=x[:, c, 0:Tt], scalar=P("gf", c), in1=rstd[:, 0:Tt],
                    op0=ALU.mult, op1=ALU.mult), reads=[("x", c), "rstd", "par"], writes=[("x", c)])
            for b in range(NBt):
                oslot = outctr[0] % 2
                outctr[0] += 1
                for half in range(2):
                    bank = mmbank()
                    def otr(e, half=half, bank=bank, b=b):
                        ins = None
                        for cc in range(4):
                            c = half * 4 + cc
                            ins = e.transpose(ps[:, bank, cc * 128:(cc + 1) * 128], x[:, c, b * 128:(b + 1) * 128], ident)
                        return ins
                    S.op("pe", otr, reads=[("x", half * 4 + cc) for cc in range(4)] + ["par"], writes=[("ps", bank)])
                    dst = yout[:, oslot, half * 512:(half + 1) * 512]
                    if half == 0:
                        S.op("act", lambda e, dst=dst, bank=bank: e.activation(out=dst, in_=ps[:, bank, :], func=AF.Copy),
                             reads=[("ps", bank)], writes=[("yout", oslot, half)])
                    else:
                        S.op("dve", lambda e, dst=dst, bank=bank: e.tensor_copy(out=dst, in_=ps[:, bank, :]),
                             reads=[("ps", bank)], writes=[("yout", oslot, half)])
                row = (ti - 1) * 512 + b * 128
                S.op("act", lambda e, oslot=oslot, row=row: e.dma_start(out=out_d[row:row + 128, :], in_=yout[:, oslot, :]),
                     reads=[("yout", oslot, 0), ("yout", oslot, 1)], sem=f"o{oslot}", inc=16)

        for ti in range(NTILES):
            do_tile(ti)

        fin = [(s, S.semval[s]) for s in ("o0", "o1", "dbg") if s in S.semval]
        S.final_waits("act", fin)
        with nc.Block() as block:
            S.emit(block, sems)
    return nc


def prepare_inputs(x, mix_norm_g, w_in, b_in, sinks, conv_w, conv_b, conv_ln_g, conv_ln_b,
                   w_attn_proj, w_conv_proj, b_conv_proj, w_out, mlp_norm_g, w_mlp1, w_mlp2, final_norm_g):
    f = lambda a: np.asarray(a, dtype=np.float32)
    x = f(x)
    pc = _perm_in_cols()
    pq = _perm_q()
    groups = []
    for l in range(L):
        groups.append(_group_layout(f(w_in[l])[:, pc], 4))
        groups.append(_group_layout(f(w_attn_proj[l])[pq, :], 8))
        groups.append(_group_layout(f(w_conv_proj[l]), 8))
        groups.append(_group_layout(f(w_out[l]), 4))
        groups.append(_group_layout(f(w_mlp1[l]), 4))
        groups.append(_group_layout(f(w_mlp2[l]), 1))
    wcat = np.ascontiguousarray(np.concatenate(groups, axis=0).reshape(L * NG * 128, GW))
    par = np.zeros((128, POFF["_n"]), np.float32)

    def put(name, arr):
        arr = np.asarray(arr, np.float32).reshape(128, -1)
        par[:, POFF[name]:POFF[name] + arr.shape[1]] = arr

    put("ident", np.eye(128, dtype=np.float32))
    put("g1", _colmajor(f(mix_norm_g)))
    put("g2", _colmajor(f(mlp_norm_g)))
    put("gf", _colmajor(f(final_norm_g)))
    put("bin", _colmajor(f(b_in)[:, pc]))
    cw = f(conv_w)
    cwp = _colmajor(cw)
    put("cw", np.transpose(cwp, (0, 1, 3, 2)))
    put("cb", _colmajor(f(conv_b)))
    put("lng", _colmajor(f(conv_ln_g)))
    put("lnb", _colmajor(f(conv_ln_b)))
    put("bcp", _colmajor(f(b_conv_proj)))
    par[:, POFF["eps"]] = EPS
    sk4 = f(sinks).reshape(L, 2, 4)
    sk = np.zeros((128, L, 4, 128), np.float32)
    for g in range(2):
        sk[g * 64:(g + 1) * 64] = sk4[None, :, g, :, None]
    sk = np.ascontiguousarray(sk.reshape(128, L * 512))
    alb = _alibi_table()
    in_maps = []
    for c in range(NCORES):
        b = c // 4
        t0 = (c % 4) * TOK_CORE
        slab = np.zeros((SLAB, D), np.float32)
        if t0 == 0:
            slab[HALO:] = x[b, 0:TOK_CORE]
        else:
            slab[:] = x[b, t0 - HALO:t0 + TOK_CORE]
        pcore = par.copy()
        pcore[:, POFF["flag"]] = 0.0 if t0 == 0 else 1.0
        in_maps.append({"xs": slab, "wcat": wcat, "par": pcore, "sk": sk, "alb": alb})
    return in_maps


def kernel(**inputs):
    in_maps = prepare_inputs(**inputs)
    nc = build_program()
    res = run_bass_kernel_spmd(nc, in_maps, core_ids=list(range(NCORES)))
    out = np.zeros((BATCH, SEQ, D), np.float32)
    for c in range(NCORES):
        b = c // 4
        t0 = (c % 4) * TOK_CORE
        out[b, t0:t0 + TOK_CORE] = res.results[c]["out"]
    return out
```
